# Optimizing a Trainium2 kernel written in Bass

```python
import math
import jax
import jax.numpy as jnp
from jax import lax
import numpy as np

D_MODEL = 1024
BATCH = 4
SEQ = 4096
DEPTH = 2
DEC_BATCH = 128
DEC_SEQ = 1
PAST_LEN = 2048
PAGE_SIZE = 128

D_MIX = D_MODEL
HEAD_DIM = 64
D_A = D_MIX // 2
H_A = D_A // HEAD_DIM
DILATED_PAIRS = ((128, 1), (512, 4), (2048, 16))
W_MAX = 2048
BLK = 128
ATTN_SCALE = HEAD_DIM ** -0.5
N_BUCKETS = 32
MAX_DIST = 2048
D_B = D_MIX // 4
G_B = 4
E_B = D_B // G_B
CHUNK = 128
D_C = D_MIX // 4
H_C = 4
E_C = D_C // H_C
CONV_W = 4
LRU_C = 8.0
PEER_HEADS = 8
N_KEYS = 128
N_EXPERTS = N_KEYS * N_KEYS
PEER_TOPK = 16
D_QUERY = 256
PEER_BLOCK = 128
D_IN = 3 * D_A + 2 * D_B + 2 * D_C
SPLITS = (D_A, 2 * D_A, 3 * D_A, 3 * D_A + D_B, 3 * D_A + 2 * D_B, 3 * D_A + 2 * D_B + D_C)
EPS = 1e-6
NEG = -1e30

kernel_name = 'hybrid_dilated_gmlp_rglru_peer_step'


def rmsnorm(x, g=None):
    xf = x.astype(jnp.float32)
    y = xf * lax.rsqrt(jnp.mean(xf * xf, axis=-1, keepdims=True) + EPS)
    if g is not None:
        y = y * g.astype(jnp.float32)
    return y.astype(x.dtype)


def t5_bucket(dist):
    exact = N_BUCKETS // 2
    df = jnp.maximum(dist, 1).astype(jnp.float32)
    large = exact + (jnp.log(df / exact) / math.log(MAX_DIST / exact) * (N_BUCKETS - exact)).astype(jnp.int32)
    return jnp.where(dist < exact, dist, jnp.minimum(large, N_BUCKETS - 1))


def _phase_blocks(x, dil, nb):
    b, s, h, e = x.shape
    L = s // dil
    xp = x.reshape(b, L, dil, h, e).transpose(0, 2, 1, 3, 4)
    xp = jnp.pad(xp, ((0, 0), (0, 0), (0, nb * BLK - L), (0, 0), (0, 0)))
    return xp.reshape(b, dil, nb, BLK, h, e)


def _with_prev(xb):
    prev = jnp.pad(xb[:, :, :-1], ((0, 0), (0, 0), (1, 0), (0, 0), (0, 0), (0, 0)))
    return jnp.concatenate([prev, xb], axis=3)


def _merge_branches(outs, lses):
    wts = jax.nn.softmax(jnp.stack(lses), axis=0)
    return jnp.einsum('rbth,rbthe->bthe', wts, jnp.stack(outs))


def dilated_attn_prompt(q, k, v, rel_bias):
    b, s, h, e = q.shape
    qf, kf, vf = q.astype(jnp.float32), k.astype(jnp.float32), v.astype(jnp.float32)
    outs, lses = [], []
    for window, dil in DILATED_PAIRS:
        L = s // dil
        nb = -(-L // BLK)
        qb = _phase_blocks(qf, dil, nb)
        kc = _with_prev(_phase_blocks(kf, dil, nb))
        vc = _with_prev(_phase_blocks(vf, dil, nb))
        qi = jnp.arange(BLK)[:, None] + BLK
        ki = jnp.arange(2 * BLK)[None, :]
        dm = qi - ki
        m_k = jnp.arange(nb)[:, None, None] * BLK - BLK + ki[None]
        valid = (dm >= 0) & (dm <= window // dil) & (m_k >= 0)
        bias = jnp.transpose(rel_bias[t5_bucket(jnp.maximum(dm, 0) * dil)], (2, 0, 1)).astype(jnp.float32)
        logits = jnp.einsum('bpnqhe,bpnkhe->bpnhqk', qb, kc) * ATTN_SCALE + bias
        logits = jnp.where(valid[:, None], logits, NEG)
        lse = jax.nn.logsumexp(logits, axis=-1)
        p = jnp.exp(logits - lse[..., None])
        o = jnp.einsum('bpnhqk,bpnkhe->bpnqhe', p, vc)
        o = o.reshape(b, dil, nb * BLK, h, e)[:, :, :L].transpose(0, 2, 1, 3, 4).reshape(b, s, h, e)
        lse = lse.transpose(0, 1, 2, 4, 3).reshape(b, dil, nb * BLK, h)[:, :, :L]
        lse = lse.transpose(0, 2, 1, 3).reshape(b, s, h)
        outs.append(o)
        lses.append(lse)
    return _merge_branches(outs, lses)


def dilated_attn_sample(q, k_all, v_all, rel_bias):
    b, t, h, e = q.shape
    w_c = k_all.shape[1] - t
    qf = q.astype(jnp.float32)
    outs, lses = [], []
    for window, dil in DILATED_PAIRS:
        j = jnp.arange(window // dil + 1)
        idx = (w_c + jnp.arange(t))[:, None] - j[None, :] * dil
        valid = idx >= 0
        idx = jnp.maximum(idx, 0)
        kg = k_all[:, idx].astype(jnp.float32)
        vg = v_all[:, idx].astype(jnp.float32)
        bias = rel_bias[t5_bucket(j * dil)].T.astype(jnp.float32)
        logits = jnp.einsum('bthe,btjhe->bthj', qf, kg) * ATTN_SCALE + bias
        logits = jnp.where(valid[:, None, :], logits, NEG)
        lse = jax.nn.logsumexp(logits, axis=-1)
        p = jnp.exp(logits - lse[..., None])
        outs.append(jnp.einsum('bthj,btjhe->bthe', p, vg))
        lses.append(lse)
    return _merge_branches(outs, lses)


def spatial_gate(u, v, w_s, b_s):
    n = v.shape[2]
    w = jnp.tril(w_s[:, :n, :n])
    mix = jnp.einsum('gij,bcjge->bcige', w, v) + b_s[:, :n].T[None, None, :, :, None]
    return u * mix


def linear_scan(a, bterm, h0):
    bterm = bterm.at[:, 0].add(a[:, 0] * h0)

    def combine(left, right):
        a_l, b_l = left
        a_r, b_r = right
        return a_l * a_r, a_r * b_l + b_r

    _, h = lax.associative_scan(combine, (a, bterm), axis=1)
    return h


def rglru_block(xr, xg, conv_buf, h0, conv_w, conv_b, w_a, b_a, w_x, b_x, lam):
    b, t, _ = xr.shape
    xp = jnp.concatenate([conv_buf.astype(xr.dtype), xr], axis=1)
    xc = conv_b + sum(xp[:, k:k + t] * conv_w[k] for k in range(CONV_W))
    new_buf = xp[:, t:]
    xh = xc.reshape(b, t, H_C, E_C)
    r = jax.nn.sigmoid((jnp.einsum('bthi,hij->bthj', xh, w_a).reshape(b, t, D_C) + b_a).astype(jnp.float32))
    ig = jax.nn.sigmoid((jnp.einsum('bthi,hij->bthj', xh, w_x).reshape(b, t, D_C) + b_x).astype(jnp.float32))
    log_a = -LRU_C * r * jax.nn.softplus(-lam.astype(jnp.float32))
    a = jnp.exp(log_a)
    bterm = jnp.sqrt(-jnp.expm1(2.0 * log_a)) * ig * xc.astype(jnp.float32)
    h = linear_scan(a, bterm, h0.astype(jnp.float32))
    y = jax.nn.gelu(xg.astype(jnp.float32)) * h
    return y.astype(xr.dtype), new_buf, h[:, -1].astype(xr.dtype)


def peer_ffn(h, w_q, sub_keys, u_tab, v_tab):
    b, t, d = h.shape
    n = b * t
    x = h.reshape(n, d)
    q = rmsnorm((x @ w_q).reshape(n, PEER_HEADS, 2, D_QUERY // 2)).astype(jnp.float32)
    s = jnp.einsum('nhpe,hpke->nhpk', q, sub_keys.astype(jnp.float32))
    v1, i1 = lax.top_k(s[:, :, 0], PEER_TOPK)
    v2, i2 = lax.top_k(s[:, :, 1], PEER_TOPK)
    cand = (v1[..., :, None] + v2[..., None, :]).reshape(n, PEER_HEADS, PEER_TOPK * PEER_TOPK)
    sv, ci = lax.top_k(cand, PEER_TOPK)
    e_idx = (jnp.take_along_axis(i1, ci // PEER_TOPK, axis=-1) * N_KEYS
             + jnp.take_along_axis(i2, ci % PEER_TOPK, axis=-1))
    g = jax.nn.softmax(sv, axis=-1)
    nk = PEER_HEADS * PEER_TOPK
    nblk = -(-n // PEER_BLOCK)
    pad = nblk * PEER_BLOCK - n
    xb = jnp.pad(x, ((0, pad), (0, 0))).reshape(nblk, PEER_BLOCK, d)
    eb = jnp.pad(e_idx.reshape(n, nk), ((0, pad), (0, 0))).reshape(nblk, PEER_BLOCK, nk)
    gb = jnp.pad(g.reshape(n, nk), ((0, pad), (0, 0))).reshape(nblk, PEER_BLOCK, nk)

    def block(args):
        xx, ee, gg = args
        act = jax.nn.gelu(jnp.einsum('nd,nkd->nk', xx.astype(jnp.float32), u_tab[ee].astype(jnp.float32)))
        return jnp.einsum('nk,nkd->nd', gg * act, v_tab[ee].astype(jnp.float32))

    out = lax.map(block, (xb, eb, gb)).reshape(nblk * PEER_BLOCK, d)[:n]
    return out.reshape(b, t, d).astype(h.dtype)


def trunk_layer(x, c, lp, rel_bias, cache_k, cache_v, conv_buf, h0):
    b, t, _ = x.shape
    sh1, sc1, g1, sh2, sc2, g2 = jnp.split((jax.nn.silu(c) @ lp['w_ada'] + lp['b_ada'])[:, None, :], 6, axis=-1)
    h = rmsnorm(x, lp['norm_mix']) * (1.0 + sc1) + sh1
    qa, ka, va, ub, vb, xr, xg = jnp.split(h @ lp['w_in'], SPLITS, axis=-1)
    q = rmsnorm(qa.reshape(b, t, H_A, HEAD_DIM), lp['q_gain'])
    k = rmsnorm(ka.reshape(b, t, H_A, HEAD_DIM), lp['k_gain'])
    v = va.reshape(b, t, H_A, HEAD_DIM)
    vb = rmsnorm(vb, lp['gmlp_norm'])
    if cache_k is None:
        ya = dilated_attn_prompt(q, k, v, rel_bias)
        keep = min(W_MAX, t)
        k_rows, v_rows = k[:, t - keep:], v[:, t - keep:]
        nc, n = t // CHUNK, CHUNK
        gv_rows = vb[:, t - CHUNK:]
    else:
        k_all = jnp.concatenate([cache_k.astype(k.dtype), k], axis=1)
        v_all = jnp.concatenate([cache_v.astype(v.dtype), v], axis=1)
        ya = dilated_attn_sample(q, k_all, v_all, rel_bias)
        k_rows, v_rows = k, v
        nc, n = 1, t
        gv_rows = vb
    yb = spatial_gate(ub.reshape(b, nc, n, G_B, E_B), vb.reshape(b, nc, n, G_B, E_B),
                      lp['w_s'], lp['b_s']).reshape(b, t, D_B)
    yc, new_buf, h_last = rglru_block(xr, xg, conv_buf, h0, lp['conv_w'], lp['conv_b'],
                                      lp['w_a'], lp['b_a'], lp['w_x'], lp['b_x'], lp['lru_lambda'])
    og = lp['out_gain']
    mix = jnp.concatenate([rmsnorm(ya.reshape(b, t, D_A).astype(x.dtype), og[:D_A]),
                           rmsnorm(yb, og[D_A:D_A + D_B]),
                           rmsnorm(yc, og[D_A + D_B:])], axis=-1)
    x = x + g1 * (mix @ lp['w_out'])
    h2 = rmsnorm(x, lp['norm_ffn']) * (1.0 + sc2) + sh2
    x = x + g2 * peer_ffn(h2, lp['peer_wq'], lp['peer_keys'], lp['peer_u'], lp['peer_v'])
    return x, (k_rows, v_rows, gv_rows, new_buf, h_last)


def setup_inputs(seed: int = 0) -> dict:
    key = jax.random.key(seed)
    ks = iter(jax.random.split(key, 40))

    def nrm(shape, scale):
        return scale * jax.random.normal(next(ks), shape, jnp.float32)

    w_cache = min(W_MAX, PAST_LEN)
    a0 = jax.random.uniform(next(ks), (DEPTH, D_C), jnp.float32, minval=0.9, maxval=0.999)
    return {
        'x_prompt': nrm((BATCH, SEQ, D_MODEL), 1.0),
        'x_sample': nrm((DEC_BATCH, DEC_SEQ, D_MODEL), 1.0),
        'cache_k': nrm((DEPTH, DEC_BATCH, w_cache, H_A, HEAD_DIM), 1.0),
        'cache_v': nrm((DEPTH, DEC_BATCH, w_cache, H_A, HEAD_DIM), 1.0),
        'state_conv': nrm((DEPTH, DEC_BATCH, CONV_W - 1, D_C), 1.0),
        'state_h': nrm((DEPTH, DEC_BATCH, D_C), 0.5),
        'c_prompt': nrm((BATCH, D_MODEL), 1.0),
        'c_sample': nrm((DEC_BATCH, D_MODEL), 1.0),
        'rel_bias': nrm((N_BUCKETS, H_A), 0.1),
        'w_ada': nrm((DEPTH, D_MODEL, 6 * D_MODEL), 0.5 * D_MODEL ** -0.5),
        'b_ada': nrm((DEPTH, 6 * D_MODEL), 0.01),
        'norm_mix': 1.0 + nrm((DEPTH, D_MODEL), 0.05),
        'norm_ffn': 1.0 + nrm((DEPTH, D_MODEL), 0.05),
        'w_in': nrm((DEPTH, D_MODEL, D_IN), D_MODEL ** -0.5),
        'q_gain': 1.0 + nrm((DEPTH, HEAD_DIM), 0.05),
        'k_gain': 1.0 + nrm((DEPTH, HEAD_DIM), 0.05),
        'gmlp_norm': 1.0 + nrm((DEPTH, D_B), 0.05),
        'w_s': nrm((DEPTH, G_B, CHUNK, CHUNK), CHUNK ** -0.5),
        'b_s': 1.0 + nrm((DEPTH, G_B, CHUNK), 0.05),
        'conv_w': nrm((DEPTH, CONV_W, D_C), CONV_W ** -0.5),
        'conv_b': nrm((DEPTH, D_C), 0.01),
        'w_a': nrm((DEPTH, H_C, E_C, E_C), E_C ** -0.5),
        'b_a': nrm((DEPTH, D_C), 0.01),
        'w_x': nrm((DEPTH, H_C, E_C, E_C), E_C ** -0.5),
        'b_x': nrm((DEPTH, D_C), 0.01),
        'lru_lambda': jnp.log(a0) - jnp.log1p(-a0),
        'out_gain': 1.0 + nrm((DEPTH, D_MIX), 0.05),
        'w_out': nrm((DEPTH, D_MIX, D_MODEL), D_MIX ** -0.5),
        'peer_wq': nrm((DEPTH, D_MODEL, PEER_HEADS * D_QUERY), D_MODEL ** -0.5),
        'peer_keys': nrm((DEPTH, PEER_HEADS, 2, N_KEYS, D_QUERY // 2), (D_QUERY // 2) ** -0.5),
        'peer_u': nrm((DEPTH, N_EXPERTS, D_MODEL), D_MODEL ** -0.5),
        'peer_v': nrm((DEPTH, N_EXPERTS, D_MODEL), (PEER_HEADS * PEER_TOPK) ** -0.5),
    }


def reference(x_prompt, x_sample, cache_k, cache_v, state_conv, state_h, c_prompt, c_sample,
              rel_bias, w_ada, b_ada, norm_mix, norm_ffn, w_in, q_gain, k_gain, gmlp_norm,
              w_s, b_s, conv_w, conv_b, w_a, b_a, w_x, b_x, lru_lambda, out_gain, w_out,
              peer_wq, peer_keys, peer_u, peer_v):
    xp, xs = x_prompt, x_sample
    zeros_buf = jnp.zeros((x_prompt.shape[0], CONV_W - 1, D_C), x_prompt.dtype)
    zeros_h = jnp.zeros((x_prompt.shape[0], D_C), x_prompt.dtype)
    st_p, st_s = [], []
    for l in range(DEPTH):
        lp = dict(w_ada=w_ada[l], b_ada=b_ada[l], norm_mix=norm_mix[l], norm_ffn=norm_ffn[l],
                  w_in=w_in[l], q_gain=q_gain[l], k_gain=k_gain[l], gmlp_norm=gmlp_norm[l],
                  w_s=w_s[l], b_s=b_s[l], conv_w=conv_w[l], conv_b=conv_b[l], w_a=w_a[l],
                  b_a=b_a[l], w_x=w_x[l], b_x=b_x[l], lru_lambda=lru_lambda[l],
                  out_gain=out_gain[l], w_out=w_out[l], peer_wq=peer_wq[l],
                  peer_keys=peer_keys[l], peer_u=peer_u[l], peer_v=peer_v[l])
        xp, sp = trunk_layer(xp, c_prompt, lp, rel_bias, None, None, zeros_buf, zeros_h)
        xs, ss = trunk_layer(xs, c_sample, lp, rel_bias, cache_k[l], cache_v[l], state_conv[l], state_h[l])
        st_p.append(sp)
        st_s.append(ss)
    k_prompt = jnp.stack([s[0] for s in st_p])
    v_prompt = jnp.stack([s[1] for s in st_p])
    gv_prompt = jnp.stack([s[2] for s in st_p])
    conv_prompt = jnp.stack([s[3] for s in st_p])
    h_prompt = jnp.stack([s[4] for s in st_p])
    k_sample = jnp.stack([s[0] for s in st_s])
    v_sample = jnp.stack([s[1] for s in st_s])
    gv_sample = jnp.stack([s[2] for s in st_s])
    conv_sample = jnp.stack([s[3] for s in st_s])
    h_sample = jnp.stack([s[4] for s in st_s])
    return (xp, xs, k_prompt, v_prompt, k_sample, v_sample, gv_prompt, gv_sample,
            conv_prompt, conv_sample, h_prompt, h_sample)
```

```python
import math
import numpy as np
from contextlib import ExitStack
import concourse.bass as bass
import concourse.mybir as mybir
from concourse.bass_utils import run_bass_kernel_spmd

F32 = mybir.dt.float32
BF16 = mybir.dt.bfloat16
I32 = mybir.dt.int32
U32 = mybir.dt.uint32
AF = mybir.ActivationFunctionType
ALU = mybir.AluOpType
AX = mybir.AxisListType

D = 1024
SEQ = 4096
NT = 33
TOK = NT * 128
NS = 16
DEPTH = 2
DEPTH_RUN = 2
EPS = 1e-6
DIN = 2560
DILS = (1, 4, 16)
STAGES = dict(ada=True, s1=True, lru=True, attn=True, s3=True, peer=True)


class T:
    def __init__(self, ap, name=""):
        self.ap = ap
        self.name = name
        self.w = {}
        self.r = {}

    def __getitem__(self, k):
        return self.ap[k]


class Ctx:
    NDS = 24

    def __init__(self, nc, es):
        self.nc = nc
        self.es = es
        self.eng = {'pe': nc.tensor, 'act': nc.scalar, 'dve': nc.vector, 'pool': nc.gpsimd, 'sp': nc.sync}
        self.csem = {k: es.enter_context(nc.semaphore("c_" + k)) for k in ['pe', 'act', 'dve', 'pool']}
        self.cnt = {k: 0 for k in self.csem}
        self.waited = {e: {} for e in self.eng}
        self.dsem = {q: [es.enter_context(nc.semaphore(f"d_{q}{i}")) for i in range(self.NDS)] for q in ['sp', 'pool']}
        self.dcount = {'sp': 0, 'pool': 0}
        self.ninst = 0

    def sb(self, name, shape, dt=F32):
        return T(self.es.enter_context(self.nc.sbuf_tensor(name, shape, dt)), name)

    def ps(self, name, shape, dt=F32):
        return T(self.es.enter_context(self.nc.psum_tensor(name, shape, dt)), name)

    def _wait(self, eng, deps, own=None, keep_last=False):
        e = self.eng[eng]
        wd = self.waited[eng]
        need = {}
        for (sem, val) in deps:
            if own is not None and sem is own:
                continue
            k = id(sem)
            if wd.get(k, 0) >= val:
                continue
            if k not in need or need[k][1] < val:
                need[k] = (sem, val)
        need = list(need.values())
        last = None
        if keep_last and need:
            last = need.pop()
        for (sem, val) in need:
            e.wait_ge(sem, val)
            wd[id(sem)] = val
        if last is not None:
            wd[id(last[0])] = last[1]
        return last

    def _mark(self, tok, reads, writes):
        k = id(tok[0])
        for t in reads:
            t.r[k] = tok
        for t in writes:
            t.w[k] = tok
            t.r = {}

    def op(self, eng, fn, reads=(), writes=()):
        sem = self.csem[eng]
        deps = []
        for t in reads:
            deps.extend(t.w.values())
        for t in writes:
            deps.extend(v for v in t.w.values() if v[0] is not sem)
            deps.extend(v for v in t.r.values() if v[0] is not sem)
        last = self._wait(eng, deps, own=(sem if eng == 'pe' else None), keep_last=True)
        ins = fn(self.eng[eng])
        if last is not None:
            ins._wait_ge(last[0], last[1])
        self.cnt[eng] += 1
        ins.then_inc(sem, 1)
        self.ninst += 1
        self._mark((sem, self.cnt[eng]), reads, writes)
        return ins

    def dma(self, q, out, in_, reads=(), writes=(), indirect=None, **kw):
        pool = self.dsem[q]
        i = self.dcount[q]
        s = pool[i % self.NDS]
        val = 16 * (i // self.NDS + 1)
        deps = []
        for t in reads:
            deps.extend(t.w.values())
        for t in writes:
            deps.extend(t.w.values())
            deps.extend(t.r.values())
        if i >= self.NDS:
            deps.append((s, val - 16))
        last = self._wait(q, deps, keep_last=True)
        e = self.eng[q]
        if indirect is None:
            ins = e.dma_start(out=out, in_=in_, **kw)
        else:
            ins = e.indirect_dma_start(out=out, out_offset=None, in_=in_, in_offset=indirect, **kw)
        if last is not None:
            ins._wait_ge(last[0], last[1])
        ins.then_inc(s, 16)
        self.dcount[q] += 1
        self.ninst += 1
        self._mark((s, val), reads, writes)
        return ins

    def finish(self):
        alltok = []
        for k, sem in self.csem.items():
            if self.cnt[k]:
                alltok.append((sem, self.cnt[k]))
        for q in self.dsem:
            n = self.dcount[q]
            for j, s in enumerate(self.dsem[q]):
                c = (n - j + self.NDS - 1) // self.NDS if n > j else 0
                if c > 0:
                    alltok.append((s, 16 * c))
        for e in ['sp', 'act', 'dve', 'pool', 'pe']:
            self._wait(e, alltok)


def bcl(ap, n):
    return ap[:, :, None].broadcast_to([ap.shape[0], ap.shape[1], n])


def t5_bucket_np(dist):
    dist = np.asarray(dist)
    df = np.maximum(dist, 1).astype(np.float32)
    large = 16 + (np.log(df / np.float32(16)) / np.float32(math.log(2048 / 16)) * np.float32(16)).astype(np.int32)
    return np.where(dist < 16, dist, np.minimum(large, 31))


def build():
    nc = bass.Bass("TRN2", target_bir_lowering=False)

    def din(name, shape, dt=F32):
        return nc.dram_tensor(name, list(shape), dt, kind="ExternalInput").ap()

    def dout(name, shape, dt=F32):
        return nc.dram_tensor(name, list(shape), dt, kind="ExternalOutput").ap()

    def dscr(name, shape, dt=F32):
        return nc.dram_tensor(name, list(shape), dt, kind="Internal").ap()

    xin = din("xin", [TOK, D])
    cc = din("cc", [32, D])
    cache_k = din("cache_k", [DEPTH, NS, 2048, 512])
    cache_v = din("cache_v", [DEPTH, NS, 2048, 512])
    state_conv = din("state_conv", [DEPTH, NS, 768])
    state_h = din("state_h", [DEPTH, NS, 256])
    rel_bias = din("rel_bias", [32, 8])
    w_ada = din("w_ada", [DEPTH, D, 6 * D])
    b_ada = din("b_ada", [DEPTH, 6 * D])
    norm_mix = din("norm_mix", [DEPTH, D])
    norm_ffn = din("norm_ffn", [DEPTH, D])
    w_in = din("w_in", [DEPTH, D, DIN])
    q_gain = din("q_gain", [DEPTH, 64])
    k_gain = din("k_gain", [DEPTH, 64])
    gmlp_norm = din("gmlp_norm", [DEPTH, 256])
    w_s = din("w_s", [DEPTH, 4, 128, 128])
    b_s = din("b_s", [DEPTH, 4, 128])
    conv_w = din("conv_w", [DEPTH, 4, 256])
    conv_b = din("conv_b", [DEPTH, 256])
    w_a = din("w_a", [DEPTH, 4, 64, 64])
    b_a = din("b_a", [DEPTH, 256])
    w_x = din("w_x", [DEPTH, 4, 64, 64])
    b_x = din("b_x", [DEPTH, 256])
    lru_lambda = din("lru_lambda", [DEPTH, 256])
    out_gain = din("out_gain", [DEPTH, D])
    w_out = din("w_out", [DEPTH, D, D])
    peer_wq = din("peer_wq", [DEPTH, D, 2048])
    peer_keys = din("peer_keys", [DEPTH, 16, 128, 128])
    peer_u = [din(f"peer_u{i}", [16384, D]) for i in range(DEPTH)]
    peer_v = [din(f"peer_v{i}", [16384, D]) for i in range(DEPTH)]
    ohb = din("ohb", [3, 32, 129])
    ohbr = din("ohbr", [3, 32, 128])
    y_out = dout("y_out", [TOK, D])
    k_p = dout("k_p", [DEPTH, 2048, 512])
    v_p = dout("v_p", [DEPTH, 2048, 512])
    k_s = dout("k_s", [DEPTH, NS, 512])
    v_s = dout("v_s", [DEPTH, NS, 512])
    gv_p = dout("gv_p", [DEPTH, 128, 256])
    gv_s = dout("gv_s", [DEPTH, NS, 256])
    conv_p = dout("conv_p", [DEPTH, 3, 256])
    conv_s = dout("conv_s", [DEPTH, NS, 3, 256])
    h_p = dout("h_p", [DEPTH, 256])
    h_s = dout("h_s", [DEPTH, NS, 256])
    ADA = dscr("ADA", [DEPTH, 32, 6 * D])
    X1 = dscr("X1", [TOK, D])
    XL = dscr("XL", [TOK, D])
    QT = dscr("QT", [4, 128, SEQ], BF16)
    KT = dscr("KT", [4, 128, SEQ], BF16)
    VS = dscr("VS", [SEQ, 8 * 65], BF16)
    YB = dscr("YB", [TOK, 256])
    XRT = dscr("XRT", [2, 128, TOK])
    GXT = dscr("GXT", [2, 128, TOK])
    SQKV = dscr("SQKV", [3, NS, 512])
    OB = dscr("OB", [3, SEQ, 8 * 65])
    YAS = dscr("YAS", [NS, 512])
    YCT = dscr("YCT", [2, 128, TOK], BF16)
    EB = dscr("EB", [128, 384])

    es = ExitStack()
    with es:
        c = Ctx(nc, es)
        tD = {n: T(None, n) for n in ["ADA", "X1", "XL", "QT", "KT", "VS", "YB", "XRT", "GXT", "SQKV", "OB", "YAS", "YCT", "EB", "OUT"]}
        ID = c.sb("ID", [128, 128])
        IDB = c.sb("IDB", [128, 128], BF16)
        EPSB = c.sb("EPSB", [128, 1])
        ONES = c.sb("ONES", [128, 1])
        TRIL = c.sb("TRIL", [128, 128])
        WB = c.sb("WB", [128, 8 * 3072], BF16)
        MOD = c.sb("MOD", [128, 4 * D])
        PB = [c.ps(f"PB{i}", [128, 512]) for i in range(7)]
        PT = c.ps("PT", [128, 1024], BF16)
        XT = [c.sb(f"XT{i}", [128, D]) for i in range(2)]
        STG = XT
        TMP = [c.sb(f"TMP{i}", [128, D]) for i in range(2)]
        HB = c.sb("HB", [128, D], BF16)
        HT = c.sb("HT", [128, D], BF16)
        SM = c.sb("SM", [128, 64])
        QG = c.sb("QG", [128, 128])
        GN = c.sb("GN", [128, 256])
        BSC = c.sb("BSC", [128, 8])
        WST = c.sb("WST", [128, 4 * 128])
        S2 = c.sb("S2", [128, 2048])
        QKV = S2
        QKB = c.sb("QKB", [128, 1024], BF16)
        VB = c.sb("VB", [128, 8 * 65], BF16)
        QKT = c.sb("QKT", [128, 1024], BF16)
        UV = c.sb("UV", [128, 1024])
        YBT = c.sb("YBT", [128, 256])
        FT = c.sb("FT", [128, 512])

        c.op('pool', lambda e: e.memset(ID[:], 0.0), writes=[ID])
        c.op('pool', lambda e: e.affine_select(out=ID[:], in_=ID[:], pattern=[[-1, 128]], compare_op=ALU.not_equal, fill=1.0, base=0, channel_multiplier=1), reads=[ID], writes=[ID])
        c.op('dve', lambda e: e.tensor_copy(IDB[:], ID[:]), reads=[ID], writes=[IDB])
        c.op('pool', lambda e: e.memset(EPSB[:], EPS), writes=[EPSB])
        c.op('pool', lambda e: e.memset(ONES[:], 1.0), writes=[ONES])
        c.op('pool', lambda e: e.memset(TRIL[:], 1.0), writes=[TRIL])
        c.op('pool', lambda e: e.affine_select(out=TRIL[:], in_=TRIL[:], pattern=[[1, 128]], compare_op=ALU.is_ge, fill=0.0, base=0, channel_multiplier=-1), reads=[TRIL], writes=[TRIL])
        c.op('pool', lambda e: e.memset(VB[:], 1.0), writes=[VB])

        def rstd_from_ss(ss_ap, out_ap, n, tiles):
            c.op('act', lambda e: e.activation(out=out_ap, in_=ss_ap, func=AF.Sqrt, scale=1.0 / n, bias=EPSB[0:ss_ap.shape[0], 0:1]), reads=tiles + [EPSB], writes=tiles)
            c.op('dve', lambda e: e.reciprocal(out=out_ap, in_=out_ap), reads=tiles, writes=tiles)

        ONESR = c.sb("ONESR", [128, 32])
        c.op('pool', lambda e: e.memset(ONESR[:], 1.0), writes=[ONESR])
        BSS = c.sb("BSS", [128, 8])
        EV = [c.sb(f"EV{i}", [128, 512]) for i in range(2)]
        CW = c.sb("CW", [128, 16])
        BAX = c.sb("BAX", [128, 4])
        LAM = c.sb("LAM", [128, 8])
        WAB = c.sb("WAB", [128, 4 * 128])
        XRC = c.sb("XRC", [128, 515])
        LB = c.sb("LB", [128, 4096])
        LBT = [T(LB[:, i * 512:(i + 1) * 512], f"LB{i}") for i in range(8)]
        GX, XC, RR, IG, AA, BT, HH = LBT[0:7]
        YCB = c.sb("YCB", [128, 512], BF16)
        HPREV = c.sb("HPREV", [128, 2])
        SCT = TMP[1]
        SST = TMP[0]

        BIG = c.sb("BIG", [128, 8192])
        GB = [T(BIG[:, i * 1024:(i + 1) * 1024], f"GB{i}") for i in range(8)]
        QTSv = BIG[:, 0:2048].bitcast(BF16)
        KTSv = BIG[:, 2048:4096].bitcast(BF16)
        EBMv = BIG[:, 4096:7168].bitcast(BF16)
        QTSd, KTSd, EBMd = [GB[0], GB[1]], [GB[2], GB[3]], [GB[4], GB[5], GB[6]]
        VC = [c.sb(f"VC{i}", [128, 130], BF16) for i in range(2)]
        PE_ = [c.sb(f"PE{i}", [128, 256], BF16) for i in range(2)]
        PM = [c.sb(f"PM{i}", [128, 256], BF16) for i in range(2)]
        OS = [c.sb(f"OS{i}", [128, 130]) for i in range(2)]
        EFS = c.sb("EFS", [128, 24])
        E0 = c.sb("E0", [128, 8])
        RB = c.sb("RB", [128, 8])
        OHBT = c.sb("OHBT", [128, 3 * 129])
        OHBR = c.sb("OHBR", [128, 3 * 128])
        RBC = c.sb("RBC", [128, 128])
        GT = c.sb("GT", [128, 384])
        EBF = c.sb("EBF", [128, 256])
        QROW = AA
        KN = BT
        VN = HH
        KC = [GX, XC]
        VCS = [RR, IG]
        LG = c.sb("LG", [128, 64])
        LG0 = c.sb("LG0", [128, 16])
        YA8 = c.sb("YA8", [128, 640])
        DMASK = c.sb("DMASK", [128, 512])
        c.op('pool', lambda e: e.memset(DMASK[:], 1.0), writes=[DMASK])
        c.op('pool', lambda e: e.affine_select(out=DMASK[:], in_=DMASK[:], pattern=[[1, 512]], compare_op=ALU.is_ge, fill=0.0, base=0, channel_multiplier=-64), reads=[DMASK], writes=[DMASK])
        c.op('pool', lambda e: e.affine_select(out=DMASK[:], in_=DMASK[:], pattern=[[-1, 512]], compare_op=ALU.is_ge, fill=0.0, base=63, channel_multiplier=64), reads=[DMASK], writes=[DMASK])
        c.dma('sp', RB[0:32, :], rel_bias, writes=[RB])
        c.dma('sp', OHBT[0:32, :].rearrange("p (r d) -> p r d", d=129), ohb.rearrange("r p d -> p r d"), writes=[OHBT])
        c.dma('sp', OHBR[0:32, :].rearrange("p (r d) -> p r d", d=128), ohbr.rearrange("r p d -> p r d"), writes=[OHBR])
        c.op('act', lambda e: e.activation(out=E0[0:1, :], in_=RB[0:1, :], func=AF.Exp), reads=[RB], writes=[E0])
        c.op('dve', lambda e: e.tensor_scalar(out=E0[0:1, :], in0=E0[0:1, :], scalar1=3.0, scalar2=None, op0=ALU.mult), reads=[E0], writes=[E0])
        OGC = c.sb("OGC", [128, 8])
        KEYT = c.sb("KEYT", [128, 2048], BF16)
        OBT = [c.sb(f"OBT{i}", [128, 520]) for i in range(3)]
        TV = c.sb("TV", [128, 256])
        TI = c.sb("TI", [128, 256], U32)
        TIF = c.sb("TIF", [128, 256])
        SW = c.sb("SW", [128, 128])
        SVT = c.sb("SVT", [128, 128])
        CI = c.sb("CI", [128, 128], U32)
        CWK = c.sb("CWK", [128, 256])
        GWP = [c.sb(f"GW{i}", [128, 128]) for i in range(2)]
        CJ = c.sb("CJ", [128, 256], U32)
        CJF = c.sb("CJF", [128, 256])
        EF_ = c.sb("EF_", [128, 256])
        EIDXP = [c.sb(f"EIDX{i}", [128, 128], I32) for i in range(2)]
        HPREP = [c.sb(f"HPRE{i}", [128, 128]) for i in range(2)]
        IOTA16 = c.sb("IOTA16", [128, 16])
        JUNKT = c.sb("JUNK", [128, 1024], BF16)
        JUNK = JUNKT
        VR1 = c.sb("VR1", [128, 1024], BF16)
        c.op('pool', lambda e: e.iota(IOTA16[:], pattern=[[1, 16]], base=0, channel_multiplier=0, allow_small_or_imprecise_dtypes=True), writes=[IOTA16])

        def build_ebm():
            c.op('pool', lambda e: e.memset(GT[:], 0.0), writes=[GT])
            for r in range(3):
                c.op('pe', lambda e: e.matmul(PB[1][:, 0:8], lhsT=OHBR[0:32, r * 128:(r + 1) * 128], rhs=RB[0:32, :], start=True, stop=True), reads=[OHBR, RB], writes=[PB[1]])
                c.op('act', lambda e: e.activation(out=EFS[:, r * 8:(r + 1) * 8], in_=PB[1][:, 0:8], func=AF.Exp), reads=[PB[1]], writes=[EFS])
                for h in range(8):
                    a = RB[0:32, h:h + 1]
                    c.op('dve', lambda e: e.tensor_copy(RBC[0:32, :], bass.AP(tensor=a.tensor, offset=a.offset, ap=[list(a.ap[0]), [0, 128]])), reads=[RB], writes=[RBC])
                    c.op('pe', lambda e: e.matmul(PB[0][:, 0:129], lhsT=RBC[0:32, :], rhs=OHBT[0:32, r * 129:(r + 1) * 129], start=True, stop=True), reads=[RBC, OHBT], writes=[PB[0]])
                    c.op('act', lambda e: e.activation(out=GT[:, 127:256], in_=PB[0][:, 0:129], func=AF.Exp), reads=[PB[0]], writes=[GT])
                    c.dma('sp', EB, GT[:], reads=[GT], writes=[tD["EB"]])
                    c.dma('sp', EBF[:, 0:128], bass.AP(tensor=EB.tensor, offset=EB.offset + 255, ap=[[383, 128], [1, 128]]), reads=[tD["EB"]], writes=[EBF])
                    c.dma('sp', EBF[:, 128:256], bass.AP(tensor=EB.tensor, offset=EB.offset + 127, ap=[[383, 128], [1, 128]]), reads=[tD["EB"]], writes=[EBF])
                    c.op('act', lambda e: e.copy(EBMv[:, (r * 8 + h) * 256:(r * 8 + h + 1) * 256], EBF[:]), reads=[EBF], writes=EBMd)


        def rstd_from_ss(ss_ap, out_ap, n, tile):
            p = ss_ap.shape[0]
            c.op('act', lambda e: e.activation(out=out_ap, in_=ss_ap, func=AF.Sqrt, scale=1.0 / n, bias=EPSB[0:p, 0:1]), reads=[tile, EPSB], writes=[tile])
            c.op('dve', lambda e: e.reciprocal(out=out_ap, in_=out_ap), reads=[tile], writes=[tile])

        def bc_heads(tile, c0, n, nh):
            a = tile[:, c0:c0 + n]
            return bass.AP(tensor=a.tensor, offset=a.offset, ap=[list(a.ap[0]), [0, nh], [1, n]])

        for l in range(DEPTH_RUN):
            xsrc = xin if l == 0 else XL
            xsrc_t = T(None) if l == 0 else tD["XL"]
            CS = TMP[0]
            c.dma('sp', CS[0:32, :], cc, writes=[CS])
            c.op('act', lambda e: e.activation(out=CS[0:32, :], in_=CS[0:32, :], func=AF.Silu), reads=[CS], writes=[CS])
            SILT = TMP[1]
            for k in range(8):
                c.op('pe', lambda e: e.transpose(PB[0][:, k * 32:(k + 1) * 32], CS[0:32, k * 128:(k + 1) * 128], ID[0:32, 0:32]), reads=[CS, ID], writes=[PB[0]])
            c.op('act', lambda e: e.copy(SILT[:, 0:256], PB[0][:, 0:256]), reads=[PB[0]], writes=[SILT])
            BRow = HB
            for ps_ in range(6):
                c.dma('sp', UV[0:1, :], b_ada[l:l + 1, ps_ * 1024:(ps_ + 1) * 1024], writes=[UV])
                for k in range(8):
                    st = STG[k % 2]
                    c.dma('sp', st[:], w_ada[l, k * 128:(k + 1) * 128, ps_ * 1024:(ps_ + 1) * 1024], writes=[st])
                    for j in range(2):
                        c.op('pe', lambda e: e.matmul(PB[j][0:32, :], lhsT=SILT[:, k * 32:(k + 1) * 32], rhs=st[:, j * 512:(j + 1) * 512], start=(k == 0), stop=False), reads=[SILT, st], writes=[PB[j]])
                for j in range(2):
                    col = ps_ * 1024 + j * 512
                    c.op('pe', lambda e: e.matmul(PB[j][0:32, :], lhsT=ONESR[0:1, 0:32], rhs=UV[0:1, j * 512:(j + 1) * 512], start=False, stop=True), reads=[ONESR, UV], writes=[PB[j]])
                    ev = EV[j % 2]
                    c.op('act' if j % 2 else 'dve', (lambda e: e.copy(ev[0:32, :], PB[j][0:32, :])) if j % 2 else (lambda e: e.tensor_copy(ev[0:32, :], PB[j][0:32, :])), reads=[PB[j]], writes=[ev])
                    c.dma('sp', ADA[l, :, col:col + 512], ev[0:32, :], reads=[ev], writes=[tD["ADA"]])

            def load_cast(dst_tile, dst_col, src_ap, ncols, scale_ap=None, scale_tile=None):
                cnt = load_cast.cnt
                for c0 in range(0, ncols, 1024):
                    n = min(1024, ncols - c0)
                    st = STG[cnt % 2]
                    c.dma('sp', st[:, 0:n], src_ap[:, c0:c0 + n], writes=[st])
                    dst = dst_tile[:, dst_col + c0:dst_col + c0 + n]
                    if scale_ap is not None:
                        c.op('dve', lambda e: e.tensor_scalar(out=dst, in0=st[:, 0:n], scalar1=scale_ap, scalar2=None, op0=ALU.mult), reads=[st, scale_tile], writes=[dst_tile])
                    elif cnt % 2:
                        c.op('act', lambda e: e.copy(dst, st[:, 0:n]), reads=[st], writes=[dst_tile])
                    else:
                        c.op('dve', lambda e: e.tensor_copy(dst, st[:, 0:n]), reads=[st], writes=[dst_tile])
                    cnt += 1
                load_cast.cnt = cnt
            load_cast.cnt = 0
            for k in range(8):
                load_cast(WB, k * DIN, w_in[l, k * 128:(k + 1) * 128, :], DIN)

            def load_mod(sample, stage):
                c0, n = (0, 2 * D) if stage == 1 else (2 * D, 4 * D)
                if sample:
                    c.dma('sp', MOD[0:16, 0:n], ADA[l, 1:17, c0:c0 + n], reads=[tD["ADA"]], writes=[MOD])
                else:
                    c.dma('sp', MOD[:, 0:n], ADA[l, 0, c0:c0 + n].partition_broadcast(128), reads=[tD["ADA"]], writes=[MOD])
                nsrc = norm_mix if stage == 1 else norm_ffn
                sc = (D, 2 * D) if stage == 1 else (2 * D, 3 * D)
                c.dma('sp', TMP[0][:], nsrc[l].partition_broadcast(128), writes=[TMP[0]])
                c.op('dve', lambda e: e.scalar_tensor_tensor(out=MOD[:, sc[0]:sc[1]], in0=MOD[:, sc[0]:sc[1]], scalar=1.0, in1=TMP[0][:], op0=ALU.add, op1=ALU.mult), reads=[MOD, TMP[0]], writes=[MOD])

            load_mod(False, 1)
            c.dma('sp', QG[:, 0:64], q_gain[l].partition_broadcast(128), writes=[QG])
            c.dma('sp', QG[:, 64:128], k_gain[l].partition_broadcast(128), writes=[QG])
            c.op('dve', lambda e: e.tensor_scalar(out=QG[:, 0:64], in0=QG[:, 0:64], scalar1=0.125, scalar2=None, op0=ALU.mult), reads=[QG], writes=[QG])
            c.dma('sp', GN[:], gmlp_norm[l].partition_broadcast(128), writes=[GN])
            c.dma('sp', BSC[:, 0:4], b_s[l].rearrange("g i -> i g"), writes=[BSC], allow_slow_non_contiguous=True)
            c.dma('sp', BSS[:, 0:4], w_s[l, :, 0, 0].partition_broadcast(128), writes=[BSS], allow_slow_non_contiguous=True)
            c.dma('sp', BSS[:, 4:8], b_s[l, :, 0].partition_broadcast(128), writes=[BSS], allow_slow_non_contiguous=True)
            for g in range(4):
                st = EV[g % 2]
                c.dma('sp', st[:, 0:128], w_s[l, g], writes=[st])
                c.op('pe', lambda e: e.transpose(PB[6][:, 0:128], st[:, 0:128], ID[:]), reads=[st, ID], writes=[PB[6]])
                c.op('dve', lambda e: e.tensor_tensor(out=WST[:, g * 128:(g + 1) * 128], in0=PB[6][:, 0:128], in1=TRIL[:], op=ALU.mult), reads=[PB[6], TRIL], writes=[WST])

            for t in range(NT if STAGES['s1'] else 0):
                sample = (t == NT - 1)
                if sample:
                    load_mod(True, 1)
                r0 = t * 128
                xt = XT[t % 2]
                c.dma('sp', xt[:], xsrc[r0:r0 + 128, :], reads=[xsrc_t], writes=[xt])
                c.op('act', lambda e: e.activation(out=TMP[0][:], in_=xt[:], func=AF.Square, accum_out=SM[:, 0:1]), reads=[xt], writes=[TMP[0], SM])
                rstd_from_ss(SM[:, 0:1], SM[:, 1:2], D, SM)
                c.op('dve', lambda e: e.scalar_tensor_tensor(out=TMP[1][:], in0=xt[:], scalar=SM[:, 1:2], in1=MOD[:, D:2 * D], op0=ALU.mult, op1=ALU.mult), reads=[xt, SM, MOD], writes=[TMP[1]])
                c.op('pool', lambda e: e.tensor_tensor(out=HB[:], in0=TMP[1][:], in1=MOD[:, 0:D], op=ALU.add), reads=[TMP[1], MOD], writes=[HB])
                for k in range(8):
                    c.op('pe', lambda e: e.transpose(PT[:, k * 128:(k + 1) * 128], HB[:, k * 128:(k + 1) * 128], IDB[:]), reads=[HB, IDB], writes=[PT])
                c.op('act', lambda e: e.copy(HT[:], PT[:]), reads=[PT], writes=[HT])
                for jb in range(5):
                    for k in range(8):
                        c.op('pe', lambda e: e.matmul(PB[jb][:], lhsT=HT[:, k * 128:(k + 1) * 128], rhs=WB[:, k * DIN + jb * 512:k * DIN + (jb + 1) * 512], start=(k == 0), stop=(k == 7)), reads=[HT, WB], writes=[PB[jb]])
                for cg in range(4):
                    for k in range(8):
                        c.op('pe', lambda e: e.matmul(PB[5][:, cg * 128:(cg + 1) * 128], lhsT=WB[:, k * DIN + 2048 + cg * 128:k * DIN + 2048 + (cg + 1) * 128], rhs=HT[:, k * 128:(k + 1) * 128], start=(k == 0), stop=(k == 7)), reads=[HT, WB], writes=[PB[5]])
                for qi in range(2):
                    pb = PB[qi]
                    so = 8 + 16 * qi
                    c.op('act', lambda e: e.activation(out=TMP[0][:, 0:512], in_=pb[:], func=AF.Square), reads=[pb], writes=[TMP[0]])
                    c.op('dve', lambda e: e.tensor_reduce(out=SM[:, so:so + 8], in_=TMP[0][:, 0:512].rearrange("p (h e) -> p h e", e=64), axis=AX.X, op=ALU.add), reads=[TMP[0]], writes=[SM])
                    rstd_from_ss(SM[:, so:so + 8], SM[:, so + 8:so + 16], 64, SM)
                    qv = QKV[:, qi * 512:(qi + 1) * 512].rearrange("p (h e) -> p h e", e=64)
                    c.op('dve', lambda e: e.tensor_tensor(out=qv, in0=pb[:].rearrange("p (h e) -> p h e", e=64), in1=bcl(SM[:, so + 8:so + 16], 64), op=ALU.mult), reads=[pb, SM], writes=[QKV])
                    c.op('pool', lambda e: e.tensor_tensor(out=qv, in0=qv, in1=bc_heads(QG, qi * 64, 64, 8), op=ALU.mult), reads=[QKV, QG], writes=[QKV])
                c.op('act', lambda e: e.copy(QKB[:], QKV[:, 0:1024]), reads=[QKV], writes=[QKB])
                c.op('act', lambda e: e.copy(QKV[:, 1024:1536], PB[2][:]), reads=[PB[2]], writes=[QKV])
                c.op('dve', lambda e: e.tensor_copy(VB[:].rearrange("p (h e) -> p h e", e=65)[:, :, 0:64], PB[2][:].rearrange("p (h e) -> p h e", e=64)), reads=[PB[2]], writes=[VB])
                if not sample:
                    if t >= 16:
                        c.dma('sp', k_p[l, (t - 16) * 128:(t - 15) * 128, :], QKV[:, 512:1024], reads=[QKV], writes=[tD["OUT"]])
                        c.dma('sp', v_p[l, (t - 16) * 128:(t - 15) * 128, :], QKV[:, 1024:1536], reads=[QKV], writes=[tD["OUT"]])
                    for j in range(8):
                        c.op('pe', lambda e: e.transpose(PT[:, j * 128:(j + 1) * 128], QKB[:, j * 128:(j + 1) * 128], IDB[:]), reads=[QKB, IDB], writes=[PT])
                    c.op('dve', lambda e: e.tensor_copy(QKT[:], PT[:]), reads=[PT], writes=[QKT])
                    c.dma('sp', QT[:, :, r0:r0 + 128].rearrange("h p t -> p h t"), QKT[:, 0:512].rearrange("p (h t) -> p h t", t=128), reads=[QKT], writes=[tD["QT"]])
                    c.dma('sp', KT[:, :, r0:r0 + 128].rearrange("h p t -> p h t"), QKT[:, 512:1024].rearrange("p (h t) -> p h t", t=128), reads=[QKT], writes=[tD["KT"]])
                    c.dma('sp', VS[r0:r0 + 128, :], VB[:], reads=[VB], writes=[tD["VS"]])
                else:
                    c.dma('sp', k_s[l], QKV[0:16, 512:1024], reads=[QKV], writes=[tD["OUT"]])
                    c.dma('sp', v_s[l], QKV[0:16, 1024:1536], reads=[QKV], writes=[tD["OUT"]])
                    c.dma('sp', SQKV.rearrange("a s d -> s a d"), QKV[0:16, 0:1536].rearrange("p (a d) -> p a d", d=512), reads=[QKV], writes=[tD["SQKV"]])
                c.op('act', lambda e: e.copy(UV[:, 0:512], PB[3][:]), reads=[PB[3]], writes=[UV])
                c.op('act', lambda e: e.activation(out=TMP[0][:, 0:256], in_=UV[:, 256:512], func=AF.Square, accum_out=SM[:, 2:3]), reads=[UV], writes=[TMP[0], SM])
                rstd_from_ss(SM[:, 2:3], SM[:, 3:4], 256, SM)
                c.op('dve', lambda e: e.scalar_tensor_tensor(out=UV[:, 256:512], in0=UV[:, 256:512], scalar=SM[:, 3:4], in1=GN[:], op0=ALU.mult, op1=ALU.mult), reads=[UV, SM, GN], writes=[UV])
                if t == NT - 2:
                    c.dma('sp', gv_p[l], UV[:, 256:512], reads=[UV], writes=[tD["OUT"]])
                if sample:
                    c.dma('sp', gv_s[l], UV[0:16, 256:512], reads=[UV], writes=[tD["OUT"]])
                    for g in range(4):
                        c.op('dve', lambda e: e.tensor_scalar(out=YBT[:, g * 64:(g + 1) * 64], in0=UV[:, 256 + g * 64:256 + (g + 1) * 64], scalar1=BSS[:, g:g + 1], scalar2=BSS[:, 4 + g:5 + g], op0=ALU.mult, op1=ALU.add), reads=[UV, BSS], writes=[YBT])
                    c.op('pool', lambda e: e.tensor_tensor(out=YBT[:], in0=YBT[:], in1=UV[:, 0:256], op=ALU.mult), reads=[YBT, UV], writes=[YBT])
                else:
                    for g in range(4):
                        c.op('pe', lambda e: e.matmul(PB[6][:, g * 64:(g + 1) * 64], lhsT=WST[:, g * 128:(g + 1) * 128], rhs=UV[:, 256 + g * 64:256 + (g + 1) * 64], start=True, stop=True), reads=[WST, UV], writes=[PB[6]])
                    for g in range(4):
                        c.op('dve', lambda e: e.scalar_tensor_tensor(out=YBT[:, g * 64:(g + 1) * 64], in0=PB[6][:, g * 64:(g + 1) * 64], scalar=BSC[:, g:g + 1], in1=UV[:, g * 64:(g + 1) * 64], op0=ALU.add, op1=ALU.mult), reads=[PB[6], BSC, UV], writes=[YBT])
                c.dma('sp', YB[r0:r0 + 128, :], YBT[:], reads=[YBT], writes=[tD["YB"]])
                if t >= NT - 2:
                    c.op('act', lambda e: e.copy(UV[:, 512:1024], PB[4][:]), reads=[PB[4]], writes=[UV])
                    if sample:
                        c.dma('sp', conv_s[l, :, 2, :], UV[0:16, 512:768], reads=[UV], writes=[tD["OUT"]])
                        c.dma('sp', conv_s[l, :, 0:2, :], state_conv[l, :, 256:768].rearrange("s (k c) -> s k c", c=256), writes=[tD["OUT"]])
                    else:
                        c.dma('sp', conv_p[l], UV[125:128, 512:768], reads=[UV], writes=[tD["OUT"]])
                c.op('dve', lambda e: e.tensor_copy(FT[:, 0:256], PB[5][:, 0:256]), reads=[PB[5]], writes=[FT])
                c.op('act', lambda e: e.activation(out=FT[:, 256:512], in_=PB[5][:, 256:512], func=AF.Gelu), reads=[PB[5]], writes=[FT])
                c.dma('sp', XRT[:, :, r0:r0 + 128].rearrange("c p t -> p c t"), FT[:, 0:256].rearrange("p (c t) -> p c t", t=128), reads=[FT], writes=[tD["XRT"]])
                c.dma('sp', GXT[:, :, r0:r0 + 128].rearrange("c p t -> p c t"), FT[:, 256:512].rearrange("p (c t) -> p c t", t=128), reads=[FT], writes=[tD["GXT"]])

            if STAGES['lru']:
                for cch_ in range(2):
                    for k_ in range(4):
                        c.dma('sp', CW[:, cch_ * 4 + k_:cch_ * 4 + k_ + 1], conv_w[l, k_, cch_ * 128:(cch_ + 1) * 128].rearrange("(p o) -> p o", o=1), writes=[CW], allow_slow_non_contiguous=True)
                c.dma('sp', CW[:, 8:10], conv_b[l].rearrange("(c p) -> p c", p=128), writes=[CW], allow_slow_non_contiguous=True)
                c.dma('sp', BAX[:, 0:2], b_a[l].rearrange("(c p) -> p c", p=128), writes=[BAX], allow_slow_non_contiguous=True)
                c.dma('sp', BAX[:, 2:4], b_x[l].rearrange("(c p) -> p c", p=128), writes=[BAX], allow_slow_non_contiguous=True)
                c.dma('sp', LAM[:, 0:2], lru_lambda[l].rearrange("(c p) -> p c", p=128), writes=[LAM], allow_slow_non_contiguous=True)
                c.op('act', lambda e: e.activation(out=LAM[:, 2:4], in_=LAM[:, 0:2], func=AF.Exp, scale=-1.0), reads=[LAM], writes=[LAM])
                c.op('act', lambda e: e.activation(out=LAM[:, 2:4], in_=LAM[:, 2:4], func=AF.Ln, bias=ONES[:, 0:1]), reads=[LAM, ONES], writes=[LAM])
                c.op('dve', lambda e: e.tensor_scalar(out=LAM[:, 4:6], in0=LAM[:, 2:4], scalar1=-16.0, scalar2=None, op0=ALU.mult), reads=[LAM], writes=[LAM])
                c.op('dve', lambda e: e.tensor_scalar(out=LAM[:, 2:4], in0=LAM[:, 2:4], scalar1=-8.0, scalar2=None, op0=ALU.mult), reads=[LAM], writes=[LAM])
                c.op('pool', lambda e: e.memset(WAB[:], 0.0), writes=[WAB])
                for gi, wsrc in enumerate((w_a, w_x)):
                    for h in range(4):
                        cch, hh = h // 2, h % 2
                        base = (gi * 2 + cch) * 128
                        c.dma('sp', WAB[hh * 64:(hh + 1) * 64, base + hh * 64:base + (hh + 1) * 64], wsrc[l, h], writes=[WAB])
                c.op('pool', lambda e: e.memset(SST[:], 0.0), writes=[SST])
                c.dma('sp', SST[0:16, 0:768], state_conv[l], writes=[SST])
                c.dma('sp', SST[0:16, 768:1024], state_h[l], writes=[SST])
                for j in range(8):
                    pb = PB[j % 4]
                    c.op('pe', lambda e: e.transpose(pb[:, 0:128], SST[:, j * 128:(j + 1) * 128], ID[:]), reads=[SST, ID], writes=[pb])
                    c.op('act', lambda e: e.copy(SCT[:, j * 128:(j + 1) * 128], pb[:, 0:128]), reads=[pb], writes=[SCT])
                for cch in range(2):
                    w = lambda k: CW[:, cch * 4 + k:cch * 4 + k + 1]
                    c.op('pool', lambda e: e.memset(XRC[:, 0:3], 0.0), writes=[XRC])
                    c.op('pool', lambda e: e.memset(HPREV[:, cch:cch + 1], 0.0), writes=[HPREV])
                    for tc in range(9):
                        smp = (tc == 8)
                        n = 128 if smp else 512
                        c0 = tc * 512
                        if smp:
                            c.dma('sp', XRC[:, 3:3 + n], XRT[cch, :, c0:c0 + n], reads=[tD["XRT"]], writes=[XRC])
                        else:
                            c.dma('sp', XRC[:, 3:515], XRT[cch, :, c0:c0 + 512], reads=[tD["XRT"]], writes=[XRC])
                        c.dma('sp', GX[:, 0:n], GXT[cch, :, c0:c0 + n], reads=[tD["GXT"]], writes=[GX])
                        c.op('dve', lambda e: e.tensor_scalar(out=XC[:, 0:n], in0=XRC[:, 3:3 + n], scalar1=w(3), scalar2=CW[:, 8 + cch:9 + cch], op0=ALU.mult, op1=ALU.add), reads=[XRC, CW], writes=[XC])
                        for k in range(3):
                            if smp:
                                src = SCT[:, (k * 2 + cch) * 128:(k * 2 + cch + 1) * 128]
                                rd = [SCT]
                            else:
                                src = XRC[:, k:k + n]
                                rd = [XRC]
                            c.op('dve', lambda e: e.scalar_tensor_tensor(out=XC[:, 0:n], in0=src, scalar=w(k), in1=XC[:, 0:n], op0=ALU.mult, op1=ALU.add), reads=rd + [CW, XC], writes=[XC])
                        c.op('pe', lambda e: e.matmul(PB[0][:, 0:n], lhsT=WAB[:, cch * 128:(cch + 1) * 128], rhs=XC[:, 0:n], start=True, stop=True), reads=[WAB, XC], writes=[PB[0]])
                        c.op('pe', lambda e: e.matmul(PB[1][:, 0:n], lhsT=WAB[:, (2 + cch) * 128:(3 + cch) * 128], rhs=XC[:, 0:n], start=True, stop=True), reads=[WAB, XC], writes=[PB[1]])
                        c.op('act', lambda e: e.activation(out=RR[:, 0:n], in_=PB[0][:, 0:n], func=AF.Sigmoid, bias=BAX[:, cch:cch + 1]), reads=[PB[0], BAX], writes=[RR])
                        c.op('act', lambda e: e.activation(out=IG[:, 0:n], in_=PB[1][:, 0:n], func=AF.Sigmoid, bias=BAX[:, 2 + cch:3 + cch]), reads=[PB[1], BAX], writes=[IG])
                        c.op('act', lambda e: e.activation(out=AA[:, 0:n], in_=RR[:, 0:n], func=AF.Exp, scale=LAM[:, 2 + cch:3 + cch]), reads=[RR, LAM], writes=[AA])
                        c.op('act', lambda e: e.activation(out=BT[:, 0:n], in_=RR[:, 0:n], func=AF.Exp, scale=LAM[:, 4 + cch:5 + cch]), reads=[RR, LAM], writes=[BT])
                        c.op('dve', lambda e: e.tensor_scalar(out=BT[:, 0:n], in0=BT[:, 0:n], scalar1=-1.0, scalar2=1.0, op0=ALU.mult, op1=ALU.add), reads=[BT], writes=[BT])
                        c.op('act', lambda e: e.activation(out=BT[:, 0:n], in_=BT[:, 0:n], func=AF.Sqrt), reads=[BT], writes=[BT])
                        c.op('dve', lambda e: e.tensor_tensor(out=BT[:, 0:n], in0=BT[:, 0:n], in1=IG[:, 0:n], op=ALU.mult), reads=[BT, IG], writes=[BT])
                        c.op('pool', lambda e: e.tensor_tensor(out=BT[:, 0:n], in0=BT[:, 0:n], in1=XC[:, 0:n], op=ALU.mult), reads=[BT, XC], writes=[BT])
                        if smp:
                            h0 = SCT[:, (6 + cch) * 128:(7 + cch) * 128]
                            c.op('dve', lambda e: e.tensor_tensor(out=HH[:, 0:n], in0=AA[:, 0:n], in1=h0, op=ALU.mult), reads=[AA, SCT], writes=[HH])
                            c.op('dve', lambda e: e.tensor_tensor(out=HH[:, 0:n], in0=HH[:, 0:n], in1=BT[:, 0:n], op=ALU.add), reads=[HH, BT], writes=[HH])
                            c.dma('sp', h_s[l, :, cch * 128:(cch + 1) * 128].rearrange("s c -> c s"), HH[:, 0:16], reads=[HH], writes=[tD["OUT"]], allow_slow_non_contiguous=True)
                        else:
                            c.op('dve', lambda e: e.tensor_tensor_scan(out=HH[:, 0:n], data0=AA[:, 0:n], data1=BT[:, 0:n], initial=HPREV[:, cch:cch + 1], op0=ALU.mult, op1=ALU.add), reads=[AA, BT, HPREV], writes=[HH])
                            c.op('act', lambda e: e.copy(HPREV[:, cch:cch + 1], HH[:, 511:512]), reads=[HH], writes=[HPREV])
                            c.op('pool', lambda e: e.tensor_copy(XRC[:, 0:3], XRC[:, 512:515]), reads=[XRC], writes=[XRC])
                            if tc == 7:
                                c.dma('sp', h_p[l, cch * 128:(cch + 1) * 128].rearrange("(c o) -> c o", o=1), HH[:, 511:512], reads=[HH], writes=[tD["OUT"]], allow_slow_non_contiguous=True)
                        c.op('dve', lambda e: e.tensor_tensor(out=YCB[:, 0:n], in0=GX[:, 0:n], in1=HH[:, 0:n], op=ALU.mult), reads=[GX, HH], writes=[YCB])
                        c.dma('sp', YCT[cch, :, c0:c0 + n], YCB[:, 0:n], reads=[YCB], writes=[tD["YCT"]])

            if STAGES['attn']:
                build_ebm()
                for hp in range(4):
                    c.dma('sp', QTSv, QT[hp], reads=[tD["QT"]], writes=QTSd)
                    c.dma('sp', KTSv, KT[hp], reads=[tD["KT"]], writes=KTSd)
                    blk = 0
                    for r, dil in enumerate(DILS):
                        nb = SEQ // dil // 128
                        for p in range(dil):
                            for n in range(nb):
                                base = p + dil * n * 128
                                vc = VC[n % 2]
                                vp = VC[(n + 1) % 2]
                                c.dma('sp', vc[:], bass.AP(tensor=VS.tensor, offset=VS.offset + base * 520 + hp * 130, ap=[[520 * dil, 128], [1, 130]]), reads=[tD["VS"]], writes=[vc])
                                po = PB[2 + blk % 2]
                                for hh in range(2):
                                    h = hp * 2 + hh
                                    ps0 = hh * 64
                                    pst = PB[hh]

                                    def tk(tile, b0):
                                        a = tile[ps0:ps0 + 64, b0:b0 + 1]
                                        return bass.AP(tensor=a.tensor, offset=a.offset, ap=[list(a.ap[0]), [dil, 128]])
                                    qa = tk(QTSv, base)
                                    c.op('pe', lambda e: e.matmul(pst[:, 128:256], lhsT=tk(KTSv, base), rhs=qa, start=True, stop=True), reads=KTSd + QTSd, writes=[pst])
                                    if n > 0:
                                        c.op('pe', lambda e: e.matmul(pst[:, 0:128], lhsT=tk(KTSv, base - dil * 128), rhs=qa, start=True, stop=True), reads=KTSd + QTSd, writes=[pst])
                                    lo = 0 if n > 0 else 128
                                    pe_ = PE_[hh]
                                    pm = PM[hh]
                                    c.op('act', lambda e: e.activation(out=pe_[:, lo:256], in_=pst[:, lo:256], func=AF.Exp), reads=[pst], writes=[pe_])
                                    eb = EBMv[:, (r * 8 + h) * 256 + lo:(r * 8 + h) * 256 + 256]
                                    c.op('dve' if hh == 0 else 'pool', lambda e: e.tensor_tensor(out=pm[:, lo:256], in0=pe_[:, lo:256], in1=eb, op=ALU.mult), reads=[pe_] + EBMd, writes=[pm])
                                    c.op('pe', lambda e: e.matmul(po[:, hh * 65:(hh + 1) * 65], lhsT=pm[:, 128:256], rhs=vc[:, hh * 65:(hh + 1) * 65], start=True, stop=(n == 0)), reads=[pm, vc], writes=[po])
                                    if n > 0:
                                        c.op('pe', lambda e: e.matmul(po[:, hh * 65:(hh + 1) * 65], lhsT=pm[:, 0:128], rhs=vp[:, hh * 65:(hh + 1) * 65], start=False, stop=True), reads=[pm, vp], writes=[po])
                                os_ = OS[blk % 2]
                                c.op('act', lambda e: e.copy(os_[:], po[:, 0:130]), reads=[po], writes=[os_])
                                c.dma('sp', bass.AP(tensor=OB.tensor, offset=OB.offset + r * SEQ * 520 + base * 520 + hp * 130, ap=[[520 * dil, 128], [1, 130]]), os_[:], reads=[os_], writes=[tD["OB"]])
                                blk += 1
                for s in range(NS):
                    c.dma('sp', QROW[:], SQKV[0, s].partition_broadcast(128), reads=[tD["SQKV"]], writes=[QROW])
                    c.dma('sp', KN[0:1, :], SQKV[1, s:s + 1, :], reads=[tD["SQKV"]], writes=[KN])
                    c.dma('sp', VN[0:1, :], SQKV[2, s:s + 1, :], reads=[tD["SQKV"]], writes=[VN])
                    for r, dil in enumerate(DILS):
                        kc = KC[r % 2]
                        vcs = VCS[r % 2]
                        row0 = 2048 - 128 * dil
                        c.dma('sp', kc[:], bass.AP(tensor=cache_k.tensor, offset=cache_k.offset + ((l * NS + s) * 2048 + row0) * 512, ap=[[512 * dil, 128], [1, 512]]), writes=[kc])
                        c.dma('sp', vcs[:], bass.AP(tensor=cache_v.tensor, offset=cache_v.offset + ((l * NS + s) * 2048 + row0) * 512, ap=[[512 * dil, 128], [1, 512]]), writes=[vcs])
                        c.op('dve', lambda e: e.tensor_tensor(out=kc[:], in0=kc[:], in1=QROW[:], op=ALU.mult), reads=[kc, QROW], writes=[kc])
                        c.op('dve', lambda e: e.tensor_reduce(out=LG[:, r * 8:(r + 1) * 8], in_=kc[:].rearrange("p (h e) -> p h e", e=64), axis=AX.X, op=ALU.add), reads=[kc], writes=[LG])
                        c.op('act', lambda e: e.activation(out=LG[:, 32 + r * 8:32 + (r + 1) * 8], in_=LG[:, r * 8:(r + 1) * 8], func=AF.Exp), reads=[LG], writes=[LG])
                        c.op('dve', lambda e: e.tensor_tensor(out=LG[:, 32 + r * 8:32 + (r + 1) * 8], in0=LG[:, 32 + r * 8:32 + (r + 1) * 8], in1=EFS[:, r * 8:(r + 1) * 8], op=ALU.mult), reads=[LG, EFS], writes=[LG])
                        c.op('pe', lambda e: e.matmul(PB[4][0:8, :], lhsT=LG[:, 32 + r * 8:32 + (r + 1) * 8], rhs=vcs[:], start=(r == 0), stop=False), reads=[LG, vcs], writes=[PB[4]])
                        c.op('pe', lambda e: e.matmul(PB[5][0:8, 0:1], lhsT=LG[:, 32 + r * 8:32 + (r + 1) * 8], rhs=ONES[:, 0:1], start=(r == 0), stop=False), reads=[LG, ONES], writes=[PB[5]])
                    c.op('dve', lambda e: e.tensor_tensor(out=KN[0:1, :], in0=KN[0:1, :], in1=QROW[0:1, :], op=ALU.mult), reads=[KN, QROW], writes=[KN])
                    c.op('dve', lambda e: e.tensor_reduce(out=LG0[0:1, 0:8], in_=KN[0:1, :].rearrange("p (h e) -> p h e", e=64), axis=AX.X, op=ALU.add), reads=[KN], writes=[LG0])
                    c.op('act', lambda e: e.activation(out=LG0[0:1, 8:16], in_=LG0[0:1, 0:8], func=AF.Exp), reads=[LG0], writes=[LG0])
                    c.op('dve', lambda e: e.tensor_tensor(out=LG0[0:1, 8:16], in0=LG0[0:1, 8:16], in1=E0[0:1, :], op=ALU.mult), reads=[LG0, E0], writes=[LG0])
                    c.op('pe', lambda e: e.matmul(PB[4][0:8, :], lhsT=LG0[0:1, 8:16], rhs=VN[0:1, :], start=False, stop=True), reads=[LG0, VN], writes=[PB[4]])
                    c.op('pe', lambda e: e.matmul(PB[5][0:8, 0:1], lhsT=LG0[0:1, 8:16], rhs=ONES[0:1, 0:1], start=False, stop=True), reads=[LG0, ONES], writes=[PB[5]])
                    c.op('dve', lambda e: e.tensor_tensor(out=YA8[0:8, 0:512], in0=PB[4][0:8, :], in1=DMASK[0:8, :], op=ALU.mult), reads=[PB[4], DMASK], writes=[YA8])
                    c.op('dve', lambda e: e.tensor_reduce(out=YA8[0:8, 512:576], in_=YA8[0:8, 0:512].rearrange("p (h e) -> p e h", e=64), axis=AX.X, op=ALU.add), reads=[YA8], writes=[YA8])
                    c.op('dve', lambda e: e.reciprocal(out=YA8[0:8, 576:577], in_=PB[5][0:8, 0:1]), reads=[PB[5]], writes=[YA8])
                    c.op('dve', lambda e: e.tensor_scalar(out=YA8[0:8, 512:576], in0=YA8[0:8, 512:576], scalar1=YA8[0:8, 576:577], scalar2=None, op0=ALU.mult), reads=[YA8], writes=[YA8])
                    c.dma('sp', YAS[s].rearrange("(h e) -> h e", e=64), YA8[0:8, 512:576], reads=[YA8], writes=[tD["YAS"]])

            if STAGES['s3']:
                last = (l == DEPTH_RUN - 1)
                xdst = y_out if last else XL
                xdst_t = tD["OUT"] if last else tD["XL"]
                c.dma('sp', OGC[:], out_gain[l].rearrange("(k p) -> p k", p=128), writes=[OGC], allow_slow_non_contiguous=True)
                for k in range(8):
                    load_cast(WB, k * 1024, w_out[l, k * 128:(k + 1) * 128, :], 1024, scale_ap=OGC[:, k:k + 1], scale_tile=OGC)
                for k in range(8):
                    load_cast(WB, 8192 + k * 2048, peer_wq[l, k * 128:(k + 1) * 128, :], 2048)
                for hp2 in range(16):
                    st = EV[hp2 % 2]
                    c.dma('sp', st[:, 0:128], peer_keys[l, hp2], writes=[st])
                    pb = PB[hp2 % 2]
                    c.op('pe', lambda e: e.transpose(pb[:, 0:128], st[:, 0:128], ID[:]), reads=[st, ID], writes=[pb])
                    c.op('act', lambda e: e.copy(KEYT[:, hp2 * 128:(hp2 + 1) * 128], pb[:, 0:128]), reads=[pb], writes=[KEYT])
                load_mod(False, 3)
                c.op('pool', lambda e: e.memset(S2[:, 0:768], 0.0), writes=[S2])
                H2P = [(LB[:, p_ * 1024:(p_ + 1) * 1024], [LBT[2 * p_], LBT[2 * p_ + 1]]) for p_ in range(2)]
                X1P = [(LB[:, 2048 + p_ * 1024:2048 + (p_ + 1) * 1024], [LBT[4 + 2 * p_], LBT[5 + 2 * p_]]) for p_ in range(2)]

                def front(t, p):
                    sample = (t == NT - 1)
                    if sample:
                        load_mod(True, 3)
                    r0 = t * 128
                    xt = XT[t % 2]
                    H2, H2d = H2P[p]
                    X1T, X1d = X1P[p]
                    EIDX, GW = EIDXP[p], GWP[p]
                    YAYB = S2
                    c.dma('sp', xt[:], xsrc[r0:r0 + 128, :], reads=[xsrc_t], writes=[xt])
                    if sample:
                        c.op('pool', lambda e: e.memset(YAYB[:, 0:512], 0.0), writes=[S2])
                        c.dma('sp', YAYB[0:16, 0:512], YAS, reads=[tD["YAS"]], writes=[S2])
                    else:
                        for r in range(3):
                            c.dma('sp', OBT[r][:], OB[r, r0:r0 + 128, :], reads=[tD["OB"]], writes=[OBT[r]])
                        c.op('dve', lambda e: e.tensor_tensor(out=OBT[0][:], in0=OBT[0][:], in1=OBT[1][:], op=ALU.add), reads=[OBT[0], OBT[1]], writes=[OBT[0]])
                        c.op('dve', lambda e: e.tensor_tensor(out=OBT[0][:], in0=OBT[0][:], in1=OBT[2][:], op=ALU.add), reads=[OBT[0], OBT[2]], writes=[OBT[0]])
                        o3 = OBT[0][:].rearrange("p (h e) -> p h e", e=65)
                        c.op('dve', lambda e: e.reciprocal(out=SM[:, 40:48], in_=o3[:, :, 64]), reads=[OBT[0]], writes=[SM])
                        c.op('dve', lambda e: e.tensor_tensor(out=YAYB[:, 0:512].rearrange("p (h e) -> p h e", e=64), in0=o3[:, :, 0:64], in1=bcl(SM[:, 40:48], 64), op=ALU.mult), reads=[OBT[0], SM], writes=[S2])
                    yield
                    c.dma('sp', YAYB[:, 512:768], YB[r0:r0 + 128, :], reads=[tD["YB"]], writes=[S2])
                    c.op('act', lambda e: e.activation(out=TMP[0][:, 0:512], in_=YAYB[:, 0:512], func=AF.Square, accum_out=SM[:, 4:5]), reads=[S2], writes=[TMP[0], SM])
                    rstd_from_ss(SM[:, 4:5], SM[:, 5:6], 512, SM)
                    c.op('act', lambda e: e.activation(out=TMP[0][:, 0:256], in_=YAYB[:, 512:768], func=AF.Square, accum_out=SM[:, 6:7]), reads=[S2], writes=[TMP[0], SM])
                    rstd_from_ss(SM[:, 6:7], SM[:, 7:8], 256, SM)
                    yield
                    MT = QKB
                    c.op('act', lambda e: e.copy(HB[:, 0:768], YAYB[:, 0:768]), reads=[S2], writes=[HB])
                    for k in range(6):
                        c.op('pe', lambda e: e.transpose(PT[:, k * 128:(k + 1) * 128], HB[:, k * 128:(k + 1) * 128], IDB[:]), reads=[HB, IDB], writes=[PT])
                    c.op('dve', lambda e: e.tensor_copy(MT[:, 0:768], PT[:, 0:768]), reads=[PT], writes=[MT])
                    yield
                    YCt = QKT
                    c.dma('sp', YCt[:, 0:256].rearrange("p (c t) -> p c t", t=128), YCT[:, :, r0:r0 + 128].rearrange("c p t -> p c t"), reads=[tD["YCT"]], writes=[QKT])
                    c.op('dve', lambda e: e.tensor_tensor(out=FT[:, 0:256], in0=YCt[:, 0:256], in1=YCt[:, 0:256], op=ALU.mult), reads=[QKT], writes=[FT])
                    for cch in range(2):
                        c.op('pe', lambda e: e.matmul(PB[3][:, 0:1], lhsT=FT[:, cch * 128:(cch + 1) * 128], rhs=ONES[:, 0:1], start=(cch == 0), stop=(cch == 1)), reads=[FT, ONES], writes=[PB[3]])
                    c.op('act', lambda e: e.copy(SM[:, 48:49], PB[3][:, 0:1]), reads=[PB[3]], writes=[SM])
                    rstd_from_ss(SM[:, 48:49], SM[:, 49:50], 256, SM)
                    yield
                    tt = TMP[0]
                    for half in range(2):
                        hs = slice(half * 512, (half + 1) * 512)
                        for k in range(4):
                            c.op('pe', lambda e: e.matmul(PB[0][:], lhsT=MT[:, k * 128:(k + 1) * 128], rhs=WB[:, k * 1024 + half * 512:k * 1024 + (half + 1) * 512], start=(k == 0), stop=(k == 3)), reads=[MT, WB], writes=[PB[0]])
                        for k in range(4, 6):
                            c.op('pe', lambda e: e.matmul(PB[1][:], lhsT=MT[:, k * 128:(k + 1) * 128], rhs=WB[:, k * 1024 + half * 512:k * 1024 + (half + 1) * 512], start=(k == 4), stop=(k == 5)), reads=[MT, WB], writes=[PB[1]])
                        for k in range(6, 8):
                            c.op('pe', lambda e: e.matmul(PB[2][:], lhsT=YCt[:, (k - 6) * 128:(k - 5) * 128], rhs=WB[:, k * 1024 + half * 512:k * 1024 + (half + 1) * 512], start=(k == 6), stop=(k == 7)), reads=[QKT, WB], writes=[PB[2]])
                        yield
                        c.op('dve', lambda e: e.tensor_scalar(out=tt[:, hs], in0=PB[0][:], scalar1=SM[:, 5:6], scalar2=None, op0=ALU.mult), reads=[PB[0], SM], writes=[tt])
                        c.op('dve', lambda e: e.scalar_tensor_tensor(out=tt[:, hs], in0=PB[1][:], scalar=SM[:, 7:8], in1=tt[:, hs], op0=ALU.mult, op1=ALU.add), reads=[PB[1], SM, tt], writes=[tt])
                        c.op('dve', lambda e: e.scalar_tensor_tensor(out=tt[:, hs], in0=PB[2][:], scalar=SM[:, 49:50], in1=tt[:, hs], op0=ALU.mult, op1=ALU.add), reads=[PB[2], SM, tt], writes=[tt])
                        yield
                    c.op('dve', lambda e: e.tensor_tensor(out=TMP[0][:], in0=TMP[0][:], in1=MOD[:, 0:D], op=ALU.mult), reads=[TMP[0], MOD], writes=[TMP[0]])
                    c.op('dve', lambda e: e.tensor_tensor(out=X1T, in0=TMP[0][:], in1=xt[:], op=ALU.add), reads=[TMP[0], xt], writes=X1d)
                    yield
                    c.op('act', lambda e: e.activation(out=TMP[0][:], in_=X1T, func=AF.Square, accum_out=SM[:, 50:51]), reads=X1d, writes=[TMP[0], SM])
                    rstd_from_ss(SM[:, 50:51], SM[:, 51:52], D, SM)
                    c.op('dve', lambda e: e.scalar_tensor_tensor(out=H2, in0=X1T, scalar=SM[:, 51:52], in1=MOD[:, 2 * D:3 * D], op0=ALU.mult, op1=ALU.mult), reads=X1d + [SM, MOD], writes=H2d)
                    c.op('dve', lambda e: e.tensor_tensor(out=H2, in0=H2, in1=MOD[:, D:2 * D], op=ALU.add), reads=H2d + [MOD], writes=H2d)
                    yield
                    c.op('act', lambda e: e.copy(HB[:], H2), reads=H2d, writes=[HB])
                    for k in range(8):
                        c.op('pe', lambda e: e.transpose(PT[:, k * 128:(k + 1) * 128], HB[:, k * 128:(k + 1) * 128], IDB[:]), reads=[HB, IDB], writes=[PT])
                    c.op('act', lambda e: e.copy(HT[:], PT[:]), reads=[PT], writes=[HT])
                    yield
                    QTB = QKT
                    for g4 in range(4):
                        sb_ = PB[1 + g4 % 2]
                        for i in range(4):
                            hp2 = g4 * 4 + i
                            for k in range(8):
                                c.op('pe', lambda e: e.matmul(PB[0][:, i * 128:(i + 1) * 128], lhsT=WB[:, 8192 + k * 2048 + hp2 * 128:8192 + k * 2048 + (hp2 + 1) * 128], rhs=HT[:, k * 128:(k + 1) * 128], start=(k == 0), stop=(k == 7)), reads=[WB, HT], writes=[PB[0]])
                            yield
                        c.op('act', lambda e: e.copy(QTB[:, 0:512], PB[0][:]), reads=[PB[0]], writes=[QKT])
                        c.op('act', lambda e: e.activation(out=FT[:], in_=PB[0][:], func=AF.Square), reads=[PB[0]], writes=[FT])
                        for i in range(4):
                            hp2 = g4 * 4 + i
                            c.op('pe', lambda e: e.matmul(PB[3][:, hp2:hp2 + 1], lhsT=FT[:, i * 128:(i + 1) * 128], rhs=ONES[:, 0:1], start=True, stop=True), reads=[FT, ONES], writes=[PB[3]])
                            c.op('pe', lambda e: e.matmul(sb_[:, i * 128:(i + 1) * 128], lhsT=QTB[:, i * 128:(i + 1) * 128], rhs=KEYT[:, hp2 * 128:(hp2 + 1) * 128], start=True, stop=True), reads=[QKT, KEYT], writes=[sb_])
                        c.op('dve', lambda e: e.tensor_copy(S2[:, g4 * 512:(g4 + 1) * 512], sb_[:]), reads=[sb_], writes=[S2])
                        yield
                    c.op('act', lambda e: e.copy(SM[:, 8:24], PB[3][:, 0:16]), reads=[PB[3]], writes=[SM])
                    rstd_from_ss(SM[:, 8:24], SM[:, 24:40], 128, SM)
                    yield
                    for hp2 in range(16):
                        sv = S2[:, hp2 * 128:(hp2 + 1) * 128]
                        c.op('dve', lambda e: e.max(out=TV[:, hp2 * 16:hp2 * 16 + 8], in_=sv), reads=[S2], writes=[TV])
                        c.op('dve', lambda e: e.max_index(out=TI[:, hp2 * 16:hp2 * 16 + 8], in_max=TV[:, hp2 * 16:hp2 * 16 + 8], in_values=sv), reads=[TV, S2], writes=[TI])
                        c.op('dve', lambda e: e.match_replace(out=SW[:], in_to_replace=TV[:, hp2 * 16:hp2 * 16 + 8], in_values=sv, imm_value=-1e30), reads=[TV, S2], writes=[SW])
                        yield
                        c.op('dve', lambda e: e.max(out=TV[:, hp2 * 16 + 8:hp2 * 16 + 16], in_=SW[:]), reads=[SW], writes=[TV])
                        c.op('dve', lambda e: e.max_index(out=TI[:, hp2 * 16 + 8:hp2 * 16 + 16], in_max=TV[:, hp2 * 16 + 8:hp2 * 16 + 16], in_values=SW[:]), reads=[TV, SW], writes=[TI])
                        yield
                    tv3 = TV[:].rearrange("p (a j) -> p a j", j=16)
                    c.op('dve', lambda e: e.tensor_tensor(out=tv3, in0=tv3, in1=bcl(SM[:, 24:40], 16), op=ALU.mult), reads=[TV, SM], writes=[TV])
                    c.op('dve', lambda e: e.tensor_copy(TIF[:], TI[:]), reads=[TI], writes=[TIF])
                    tv4 = TV[:].rearrange("p (h s j) -> p h s j", s=2, j=16)
                    v1 = tv4[:, :, 0, :]
                    v2 = tv4[:, :, 1, :]
                    CAND = S2
                    cand4 = CAND[:].rearrange("p (h a b) -> p h a b", a=16, b=16)
                    c.op('dve', lambda e: e.tensor_tensor(out=cand4, in0=v1[:, :, :, None].broadcast_to([128, 8, 16, 16]), in1=v2[:, :, None, :].broadcast_to([128, 8, 16, 16]), op=ALU.add), reads=[TV], writes=[S2])
                    yield
                    for h in range(8):
                        cv = CAND[:, h * 256:(h + 1) * 256]
                        c.op('dve', lambda e: e.max(out=SVT[:, h * 16:h * 16 + 8], in_=cv), reads=[S2], writes=[SVT])
                        c.op('dve', lambda e: e.max_index(out=CI[:, h * 16:h * 16 + 8], in_max=SVT[:, h * 16:h * 16 + 8], in_values=cv), reads=[SVT, S2], writes=[CI])
                        c.op('dve', lambda e: e.match_replace(out=CWK[:], in_to_replace=SVT[:, h * 16:h * 16 + 8], in_values=cv, imm_value=-1e30), reads=[SVT, S2], writes=[CWK])
                        yield
                        c.op('dve', lambda e: e.max(out=SVT[:, h * 16 + 8:h * 16 + 16], in_=CWK[:]), reads=[CWK], writes=[SVT])
                        c.op('dve', lambda e: e.max_index(out=CI[:, h * 16 + 8:h * 16 + 16], in_max=SVT[:, h * 16 + 8:h * 16 + 16], in_values=CWK[:]), reads=[SVT, CWK], writes=[CI])
                        yield
                    sv3 = SVT[:].rearrange("p (h k) -> p h k", k=16)
                    mx = SVT[:].rearrange("p (h k) -> p h k", k=16)[:, :, 0:1].broadcast_to([128, 8, 16])
                    g3 = GW[:].rearrange("p (h k) -> p h k", k=16)
                    c.op('dve', lambda e: e.tensor_tensor(out=g3, in0=sv3, in1=mx, op=ALU.subtract), reads=[SVT], writes=[GW])
                    c.op('act', lambda e: e.activation(out=GW[:], in_=GW[:], func=AF.Exp), reads=[GW], writes=[GW])
                    c.op('dve', lambda e: e.tensor_reduce(out=SM[:, 52:60], in_=g3, axis=AX.X, op=ALU.add), reads=[GW], writes=[SM])
                    c.op('dve', lambda e: e.reciprocal(out=SM[:, 52:60], in_=SM[:, 52:60]), reads=[SM], writes=[SM])
                    c.op('dve', lambda e: e.tensor_tensor(out=g3, in0=g3, in1=bcl(SM[:, 52:60], 16), op=ALU.mult), reads=[GW, SM], writes=[GW])
                    yield
                    c.op('dve', lambda e: e.tensor_single_scalar(out=CJ[:, 0:128], in_=CI[:], scalar=4, op=ALU.logical_shift_right), reads=[CI], writes=[CJ])
                    c.op('dve', lambda e: e.tensor_single_scalar(out=CJ[:, 128:256], in_=CI[:], scalar=15, op=ALU.bitwise_and), reads=[CI], writes=[CJ])
                    c.op('dve', lambda e: e.tensor_copy(CJF[:], CJ[:]), reads=[CJ], writes=[CJF])
                    tif4 = TIF[:].rearrange("p (h s j) -> p h s j", s=2, j=16)
                    EQ = TMP[0]
                    for s_ in range(2):
                        for hh4 in range(2):
                            hs = slice(hh4 * 4, hh4 * 4 + 4)
                            jf = CJF[:, s_ * 128:(s_ + 1) * 128].rearrange("p (h k) -> p h k", k=16)[:, hs, :]
                            eq4 = EQ[:].rearrange("p (h k j) -> p h k j", k=16, j=16)
                            io = IOTA16[:, 0:16]
                            iob = bass.AP(tensor=io.tensor, offset=io.offset, ap=[list(io.ap[0]), [0, 4], [0, 16], [1, 16]])
                            c.op('dve', lambda e: e.tensor_tensor(out=eq4, in0=jf[:, :, :, None].broadcast_to([128, 4, 16, 16]), in1=iob, op=ALU.is_equal), reads=[CJF, IOTA16], writes=[EQ])
                            c.op('dve', lambda e: e.tensor_tensor(out=eq4, in0=eq4, in1=tif4[:, hs, s_, :][:, :, None, :].broadcast_to([128, 4, 16, 16]), op=ALU.mult), reads=[EQ, TIF], writes=[EQ])
                            c.op('dve', lambda e: e.tensor_reduce(out=EF_[:, s_ * 128 + hh4 * 64:s_ * 128 + hh4 * 64 + 64].rearrange("p (h k) -> p h k", k=16), in_=eq4, axis=AX.X, op=ALU.add), reads=[EQ], writes=[EF_])
                            yield
                    c.op('dve', lambda e: e.scalar_tensor_tensor(out=EF_[:, 0:128], in0=EF_[:, 0:128], scalar=128.0, in1=EF_[:, 128:256], op0=ALU.mult, op1=ALU.add), reads=[EF_], writes=[EF_])
                    c.op('dve', lambda e: e.tensor_copy(EIDX[:], EF_[:, 0:128]), reads=[EF_], writes=[EIDX])
                    yield

                def back(t, p, fill):
                    r0 = t * 128
                    H2, H2d = H2P[p]
                    X1T, X1d = X1P[p]
                    EIDX, GW, HPRE = EIDXP[p], GWP[p], HPREP[p]
                    h2b = bass.AP(tensor=H2.tensor, offset=H2.offset, ap=[list(H2.ap[0]), [0, 4], [1, 1024]])
                    for g in range(32):
                        grp = GB[(g % 2) * 4:(g % 2) * 4 + 4]
                        for i in range(4):
                            sl = g * 4 + i
                            c.dma('pool', grp[i][:], peer_u[l], reads=[EIDX], writes=[grp[i]], indirect=bass.IndirectOffsetOnAxis(ap=EIDX[:, sl:sl + 1], axis=0))
                        gv = BIG[:, (g % 2) * 4096:(g % 2 + 1) * 4096].rearrange("p (s d) -> p s d", d=1024)
                        c.op('dve', lambda e: e.tensor_tensor(out=gv, in0=gv, in1=h2b, op=ALU.mult), reads=grp + H2d, writes=grp)
                        for i in range(4):
                            sl = g * 4 + i
                            c.op('act', lambda e: e.activation(out=grp[i][:], in_=grp[i][:], func=AF.Copy, accum_out=HPRE[:, sl:sl + 1]), reads=[grp[i]], writes=[grp[i], HPRE])

                    c.op('act', lambda e: e.activation(out=HPRE[:], in_=HPRE[:], func=AF.Gelu), reads=[HPRE], writes=[HPRE])
                    c.op('dve', lambda e: e.tensor_tensor(out=HPRE[:], in0=HPRE[:], in1=GW[:], op=ALU.mult), reads=[HPRE, GW], writes=[HPRE])
                    VR = [JUNKT, VR1]
                    for sl in range(128):
                        gb = GB[sl % 8]
                        vr = VR[sl % 2]
                        c.dma('pool', gb[:], peer_v[l], reads=[EIDX], writes=[gb], indirect=bass.IndirectOffsetOnAxis(ap=EIDX[:, sl:sl + 1], axis=0))
                        c.op('act', lambda e: e.activation(out=vr[:], in_=gb[:], func=AF.Copy, scale=HPRE[:, sl:sl + 1]), reads=[gb, HPRE], writes=[vr])
                        for half in range(2):
                            c.op('pe', lambda e: e.matmul(PB[5 + half][:], lhsT=IDB[:], rhs=vr[:, half * 512:(half + 1) * 512], start=(sl == 0), stop=(sl == 127)), reads=[IDB, vr], writes=[PB[5 + half]])
                        fill(1)
                    for half in range(2):
                        hs = slice(half * 512, (half + 1) * 512)
                        c.op('dve', lambda e: e.tensor_tensor(out=TMP[1][:, hs], in0=PB[5 + half][:], in1=MOD[:, 3 * D + half * 512:3 * D + (half + 1) * 512], op=ALU.mult), reads=[PB[5 + half], MOD], writes=[TMP[1]])
                    c.op('pool', lambda e: e.tensor_tensor(out=X1T, in0=X1T, in1=TMP[1][:], op=ALU.add), reads=X1d + [TMP[1]], writes=X1d)
                    c.dma('sp', xdst[r0:r0 + 128, :], X1T, reads=X1d, writes=[xdst_t])

                def advance(gen, n):
                    if gen is None:
                        return
                    for _ in range(n):
                        try:
                            next(gen)
                        except StopIteration:
                            return

                g0 = front(0, 0)
                advance(g0, 10 ** 6)
                for t in range(NT):
                    nxt = None
                    if t + 1 < NT - 1:
                        nxt = front(t + 1, (t + 1) % 2)
                    back(t, t % 2, (lambda n, _g=nxt: advance(_g, n)))
                    advance(nxt, 10 ** 6)
                    if t + 1 == NT - 1:
                        advance(front(t + 1, (t + 1) % 2), 10 ** 6)

        c.finish()
        print("ninst", c.ninst)
    return nc


_NC = None
import time as _time


def kernel(**inp):
    global _NC
    f = lambda a: np.ascontiguousarray(np.asarray(a, dtype=np.float32))
    x_prompt = f(inp["x_prompt"]); x_sample = f(inp["x_sample"])
    ohb = np.zeros((3, 32, 129), np.float32)
    for r, dil in enumerate(DILS):
        b = t5_bucket_np(np.arange(129) * dil)
        ohb[r, b, np.arange(129)] = 1.0
    shared = {}
    for k in ["rel_bias", "w_ada", "b_ada", "norm_mix", "norm_ffn", "w_in", "q_gain", "k_gain", "gmlp_norm", "w_s", "b_s",
              "conv_w", "conv_b", "w_a", "b_a", "w_x", "b_x", "lru_lambda", "out_gain", "w_out", "peer_wq"]:
        shared[k] = f(inp[k])
    shared["peer_keys"] = f(inp["peer_keys"]).reshape(DEPTH, 16, 128, 128)
    shared["ohb"] = ohb
    for i in range(DEPTH):
        shared[f"peer_u{i}"] = np.ascontiguousarray(f(inp["peer_u"])[i])
        shared[f"peer_v{i}"] = np.ascontiguousarray(f(inp["peer_v"])[i])
    shared["ohbr"] = np.ascontiguousarray(ohb[:, :, :0:-1])
    ck = f(inp["cache_k"]).reshape(DEPTH, 128, 2048, 512)
    cv = f(inp["cache_v"]).reshape(DEPTH, 128, 2048, 512)
    sc = f(inp["state_conv"]).reshape(DEPTH, 128, 768)
    sh = f(inp["state_h"])
    cp = f(inp["c_prompt"]); cs = f(inp["c_sample"])
    in_maps = []
    for c in range(8):
        b = c % 4
        xin = np.zeros((TOK, D), np.float32)
        xin[:SEQ] = x_prompt[b]
        xin[SEQ:SEQ + NS] = x_sample[c * NS:(c + 1) * NS, 0]
        cc = np.zeros((32, D), np.float32)
        cc[0] = cp[b]
        cc[1:1 + NS] = cs[c * NS:(c + 1) * NS]
        m = dict(shared)
        m.update(xin=xin, cc=cc,
                 cache_k=np.ascontiguousarray(ck[:, c * NS:(c + 1) * NS]), cache_v=np.ascontiguousarray(cv[:, c * NS:(c + 1) * NS]),
                 state_conv=np.ascontiguousarray(sc[:, c * NS:(c + 1) * NS]), state_h=np.ascontiguousarray(sh[:, c * NS:(c + 1) * NS]))
        in_maps.append(m)
    _t0 = _time.time()
    if _NC is None:
        _NC = build()
    print('[kernel] host prep+build done', round(_time.time() - _t0, 1), flush=True)
    res = run_bass_kernel_spmd(_NC, in_maps, core_ids=list(range(8)))
    R = res.results
    print('[kernel] launch done', round(_time.time() - _t0, 1), flush=True)
    y_prompt = np.stack([R[b]["y_out"][:SEQ] for b in range(4)])
    y_sample = np.concatenate([R[c]["y_out"][SEQ:SEQ + NS] for c in range(8)])[:, None, :]
    k_prompt = np.stack([R[b]["k_p"] for b in range(4)], axis=1).reshape(DEPTH, 4, 2048, 8, 64)
    v_prompt = np.stack([R[b]["v_p"] for b in range(4)], axis=1).reshape(DEPTH, 4, 2048, 8, 64)
    k_sample = np.concatenate([R[c]["k_s"] for c in range(8)], axis=1).reshape(DEPTH, 128, 1, 8, 64)
    v_sample = np.concatenate([R[c]["v_s"] for c in range(8)], axis=1).reshape(DEPTH, 128, 1, 8, 64)
    gv_prompt = np.stack([R[b]["gv_p"] for b in range(4)], axis=1)
    gv_sample = np.concatenate([R[c]["gv_s"] for c in range(8)], axis=1)[:, :, None, :]
    conv_prompt = np.stack([R[b]["conv_p"] for b in range(4)], axis=1)
    conv_sample = np.concatenate([R[c]["conv_s"] for c in range(8)], axis=1)
    h_prompt = np.stack([R[b]["h_p"] for b in range(4)], axis=1)
    h_sample = np.concatenate([R[c]["h_s"] for c in range(8)], axis=1)
    return tuple(np.ascontiguousarray(a, dtype=np.float32) for a in (
        y_prompt, y_sample, k_prompt, v_prompt, k_sample, v_sample, gv_prompt, gv_sample,
        conv_prompt, conv_sample, h_prompt, h_sample))
```

```python
import math
import numpy as np
from contextlib import ExitStack
import concourse.bass as bass
import concourse.mybir as mybir
from concourse.bass_utils import run_bass_kernel_spmd

F32 = mybir.dt.float32
BF16 = mybir.dt.bfloat16
I32 = mybir.dt.int32
U32 = mybir.dt.uint32
AF = mybir.ActivationFunctionType
ALU = mybir.AluOpType
AX = mybir.AxisListType

D = 1024
SEQ = 4096
NT = 33
TOK = NT * 128
NS = 16
DEPTH = 2
DEPTH_RUN = 2
EPS = 1e-6
DIN = 2560
DILS = (1, 4, 16)
STAGES = dict(ada=True, s1=True, lru=True, attn=True, s3=True, peer=True)


class T:
    def __init__(self, ap, name=""):
        self.ap = ap
        self.name = name
        self.w = {}
        self.r = {}

    def __getitem__(self, k):
        return self.ap[k]


class Ctx:
    NDS = 24

    def __init__(self, nc, es):
        self.nc = nc
        self.es = es
        self.eng = {'pe': nc.tensor, 'act': nc.scalar, 'dve': nc.vector, 'pool': nc.gpsimd, 'sp': nc.sync}
        self.csem = {k: es.enter_context(nc.semaphore("c_" + k)) for k in ['pe', 'act', 'dve', 'pool']}
        self.cnt = {k: 0 for k in self.csem}
        self.waited = {e: {} for e in self.eng}
        self.dsem = {q: [es.enter_context(nc.semaphore(f"d_{q}{i}")) for i in range(self.NDS)] for q in ['sp', 'pool']}
        self.dcount = {'sp': 0, 'pool': 0}
        self.ninst = 0

    def sb(self, name, shape, dt=F32):
        return T(self.es.enter_context(self.nc.sbuf_tensor(name, shape, dt)), name)

    def ps(self, name, shape, dt=F32):
        return T(self.es.enter_context(self.nc.psum_tensor(name, shape, dt)), name)

    def _wait(self, eng, deps, own=None, keep_last=False):
        e = self.eng[eng]
        wd = self.waited[eng]
        need = {}
        for (sem, val) in deps:
            if own is not None and sem is own:
                continue
            k = id(sem)
            if wd.get(k, 0) >= val:
                continue
            if k not in need or need[k][1] < val:
                need[k] = (sem, val)
        need = list(need.values())
        last = None
        if keep_last and need:
            last = need.pop()
        for (sem, val) in need:
            e.wait_ge(sem, val)
            wd[id(sem)] = val
        if last is not None:
            wd[id(last[0])] = last[1]
        return last

    def _mark(self, tok, reads, writes):
        k = id(tok[0])
        for t in reads:
            t.r[k] = tok
        for t in writes:
            t.w[k] = tok
            t.r = {}

    def op(self, eng, fn, reads=(), writes=()):
        sem = self.csem[eng]
        deps = []
        for t in reads:
            deps.extend(t.w.values())
        for t in writes:
            deps.extend(v for v in t.w.values() if v[0] is not sem)
            deps.extend(v for v in t.r.values() if v[0] is not sem)
        last = self._wait(eng, deps, own=(sem if eng == 'pe' else None), keep_last=True)
        ins = fn(self.eng[eng])
        if last is not None:
            ins._wait_ge(last[0], last[1])
        self.cnt[eng] += 1
        ins.then_inc(sem, 1)
        self.ninst += 1
        self._mark((sem, self.cnt[eng]), reads, writes)
        return ins

    def dma(self, q, out, in_, reads=(), writes=(), indirect=None, **kw):
        pool = self.dsem[q]
        i = self.dcount[q]
        s = pool[i % self.NDS]
        val = 16 * (i // self.NDS + 1)
        deps = []
        for t in reads:
            deps.extend(t.w.values())
        for t in writes:
            deps.extend(t.w.values())
            deps.extend(t.r.values())
        if i >= self.NDS:
            deps.append((s, val - 16))
        last = self._wait(q, deps, keep_last=True)
        e = self.eng[q]
        if indirect is None:
            ins = e.dma_start(out=out, in_=in_, **kw)
        else:
            ins = e.indirect_dma_start(out=out, out_offset=None, in_=in_, in_offset=indirect, **kw)
        if last is not None:
            ins._wait_ge(last[0], last[1])
        ins.then_inc(s, 16)
        self.dcount[q] += 1
        self.ninst += 1
        self._mark((s, val), reads, writes)
        return ins

    def finish(self):
        alltok = []
        for k, sem in self.csem.items():
            if self.cnt[k]:
                alltok.append((sem, self.cnt[k]))
        for q in self.dsem:
            n = self.dcount[q]
            for j, s in enumerate(self.dsem[q]):
                c = (n - j + self.NDS - 1) // self.NDS if n > j else 0
                if c > 0:
                    alltok.append((s, 16 * c))
        for e in ['sp', 'act', 'dve', 'pool', 'pe']:
            self._wait(e, alltok)


def bcl(ap, n):
    return ap[:, :, None].broadcast_to([ap.shape[0], ap.shape[1], n])


def t5_bucket_np(dist):
    dist = np.asarray(dist)
    df = np.maximum(dist, 1).astype(np.float32)
    large = 16 + (np.log(df / np.float32(16)) / np.float32(math.log(2048 / 16)) * np.float32(16)).astype(np.int32)
    return np.where(dist < 16, dist, np.minimum(large, 31))


def build():
    nc = bass.Bass("TRN2", target_bir_lowering=False)

    def din(name, shape, dt=F32):
        return nc.dram_tensor(name, list(shape), dt, kind="ExternalInput").ap()

    def dout(name, shape, dt=F32):
        return nc.dram_tensor(name, list(shape), dt, kind="ExternalOutput").ap()

    def dscr(name, shape, dt=F32):
        return nc.dram_tensor(name, list(shape), dt, kind="Internal").ap()

    xin = din("xin", [TOK, D])
    cc = din("cc", [32, D])
    cache_k = din("cache_k", [DEPTH, NS, 2048, 512])
    cache_v = din("cache_v", [DEPTH, NS, 2048, 512])
    state_conv = din("state_conv", [DEPTH, NS, 768])
    state_h = din("state_h", [DEPTH, NS, 256])
    rel_bias = din("rel_bias", [32, 8])
    w_ada = din("w_ada", [DEPTH, D, 6 * D])
    b_ada = din("b_ada", [DEPTH, 6 * D])
    norm_mix = din("norm_mix", [DEPTH, D])
    norm_ffn = din("norm_ffn", [DEPTH, D])
    w_in = din("w_in", [DEPTH, D, DIN])
    q_gain = din("q_gain", [DEPTH, 64])
    k_gain = din("k_gain", [DEPTH, 64])
    gmlp_norm = din("gmlp_norm", [DEPTH, 256])
    w_s = din("w_s", [DEPTH, 4, 128, 128])
    b_s = din("b_s", [DEPTH, 4, 128])
    conv_w = din("conv_w", [DEPTH, 4, 256])
    conv_b = din("conv_b", [DEPTH, 256])
    w_a = din("w_a", [DEPTH, 4, 64, 64])
    b_a = din("b_a", [DEPTH, 256])
    w_x = din("w_x", [DEPTH, 4, 64, 64])
    b_x = din("b_x", [DEPTH, 256])
    lru_lambda = din("lru_lambda", [DEPTH, 256])
    out_gain = din("out_gain", [DEPTH, D])
    w_out = din("w_out", [DEPTH, D, D])
    peer_wq = din("peer_wq", [DEPTH, D, 2048])
    peer_keys = din("peer_keys", [DEPTH, 16, 128, 128])
    peer_u = [din(f"peer_u{i}", [16384, D]) for i in range(DEPTH)]
    peer_v = [din(f"peer_v{i}", [16384, D]) for i in range(DEPTH)]
    ohb = din("ohb", [3, 32, 129])
    ohbr = din("ohbr", [3, 32, 128])
    y_out = dout("y_out", [TOK, D])
    k_p = dout("k_p", [DEPTH, 2048, 512])
    v_p = dout("v_p", [DEPTH, 2048, 512])
    k_s = dout("k_s", [DEPTH, NS, 512])
    v_s = dout("v_s", [DEPTH, NS, 512])
    gv_p = dout("gv_p", [DEPTH, 128, 256])
    gv_s = dout("gv_s", [DEPTH, NS, 256])
    conv_p = dout("conv_p", [DEPTH, 3, 256])
    conv_s = dout("conv_s", [DEPTH, NS, 3, 256])
    h_p = dout("h_p", [DEPTH, 256])
    h_s = dout("h_s", [DEPTH, NS, 256])
    ADA = dscr("ADA", [DEPTH, 32, 6 * D])
    X1 = dscr("X1", [TOK, D])
    XL = dscr("XL", [TOK, D])
    QT = dscr("QT", [4, 128, SEQ], BF16)
    KT = dscr("KT", [4, 128, SEQ], BF16)
    VS = dscr("VS", [SEQ, 8 * 65], BF16)
    YB = dscr("YB", [TOK, 256])
    XRT = dscr("XRT", [2, 128, TOK])
    GXT = dscr("GXT", [2, 128, TOK])
    SQKV = dscr("SQKV", [3, NS, 512])
    OB = dscr("OB", [3, SEQ, 8 * 65])
    YAS = dscr("YAS", [NS, 512])
    YCT = dscr("YCT", [2, 128, TOK], BF16)
    EB = dscr("EB", [128, 384])

    es = ExitStack()
    with es:
        c = Ctx(nc, es)
        tD = {n: T(None, n) for n in ["ADA", "X1", "XL", "QT", "KT", "VS", "YB", "XRT", "GXT", "SQKV", "OB", "YAS", "YCT", "EB", "OUT"]}
        ID = c.sb("ID", [128, 128])
        IDB = c.sb("IDB", [128, 128], BF16)
        EPSB = c.sb("EPSB", [128, 1])
        ONES = c.sb("ONES", [128, 1])
        TRIL = c.sb("TRIL", [128, 128])
        WB = c.sb("WB", [128, 8 * 3072], BF16)
        MOD = c.sb("MOD", [128, 4 * D])
        PB = [c.ps(f"PB{i}", [128, 512]) for i in range(7)]
        PT = c.ps("PT", [128, 1024], BF16)
        XT = [c.sb(f"XT{i}", [128, D]) for i in range(2)]
        STG = XT
        TMP = [c.sb(f"TMP{i}", [128, D]) for i in range(2)]
        HB = c.sb("HB", [128, D], BF16)
        HT = c.sb("HT", [128, D], BF16)
        SM = c.sb("SM", [128, 64])
        QG = c.sb("QG", [128, 128])
        GN = c.sb("GN", [128, 256])
        BSC = c.sb("BSC", [128, 8])
        WST = c.sb("WST", [128, 4 * 128])
        S2 = c.sb("S2", [128, 2048])
        QKV = S2
        QKB = c.sb("QKB", [128, 1024], BF16)
        VB = c.sb("VB", [128, 8 * 65], BF16)
        QKT = c.sb("QKT", [128, 1024], BF16)
        UV = c.sb("UV", [128, 1024])
        YBT = c.sb("YBT", [128, 256])
        FT = c.sb("FT", [128, 512])

        c.op('pool', lambda e: e.memset(ID[:], 0.0), writes=[ID])
        c.op('pool', lambda e: e.affine_select(out=ID[:], in_=ID[:], pattern=[[-1, 128]], compare_op=ALU.not_equal, fill=1.0, base=0, channel_multiplier=1), reads=[ID], writes=[ID])
        c.op('dve', lambda e: e.tensor_copy(IDB[:], ID[:]), reads=[ID], writes=[IDB])
        c.op('pool', lambda e: e.memset(EPSB[:], EPS), writes=[EPSB])
        c.op('pool', lambda e: e.memset(ONES[:], 1.0), writes=[ONES])
        c.op('pool', lambda e: e.memset(TRIL[:], 1.0), writes=[TRIL])
        c.op('pool', lambda e: e.affine_select(out=TRIL[:], in_=TRIL[:], pattern=[[1, 128]], compare_op=ALU.is_ge, fill=0.0, base=0, channel_multiplier=-1), reads=[TRIL], writes=[TRIL])
        c.op('pool', lambda e: e.memset(VB[:], 1.0), writes=[VB])

        def rstd_from_ss(ss_ap, out_ap, n, tiles):
            c.op('act', lambda e: e.activation(out=out_ap, in_=ss_ap, func=AF.Sqrt, scale=1.0 / n, bias=EPSB[0:ss_ap.shape[0], 0:1]), reads=tiles + [EPSB], writes=tiles)
            c.op('dve', lambda e: e.reciprocal(out=out_ap, in_=out_ap), reads=tiles, writes=tiles)

        ONESR = c.sb("ONESR", [128, 32])
        c.op('pool', lambda e: e.memset(ONESR[:], 1.0), writes=[ONESR])
        BSS = c.sb("BSS", [128, 8])
        EV = [c.sb(f"EV{i}", [128, 512]) for i in range(2)]
        CW = c.sb("CW", [128, 16])
        BAX = c.sb("BAX", [128, 4])
        LAM = c.sb("LAM", [128, 8])
        WAB = c.sb("WAB", [128, 4 * 128])
        XRC = c.sb("XRC", [128, 515])
        LB = c.sb("LB", [128, 4096])
        LBT = [T(LB[:, i * 512:(i + 1) * 512], f"LB{i}") for i in range(8)]
        GX, XC, RR, IG, AA, BT, HH = LBT[0:7]
        YCB = c.sb("YCB", [128, 512], BF16)
        HPREV = c.sb("HPREV", [128, 2])
        SCT = TMP[1]
        SST = TMP[0]

        BIG = c.sb("BIG", [128, 8192])
        GB = [T(BIG[:, i * 1024:(i + 1) * 1024], f"GB{i}") for i in range(8)]
        QTSv = BIG[:, 0:2048].bitcast(BF16)
        KTSv = BIG[:, 2048:4096].bitcast(BF16)
        EBMv = BIG[:, 4096:7168].bitcast(BF16)
        QTSd, KTSd, EBMd = [GB[0], GB[1]], [GB[2], GB[3]], [GB[4], GB[5], GB[6]]
        VC = [c.sb(f"VC{i}", [128, 130], BF16) for i in range(2)]
        PE_ = [c.sb(f"PE{i}", [128, 256], BF16) for i in range(2)]
        PM = [c.sb(f"PM{i}", [128, 256], BF16) for i in range(2)]
        OS = [c.sb(f"OS{i}", [128, 130]) for i in range(2)]
        EFS = c.sb("EFS", [128, 24])
        E0 = c.sb("E0", [128, 8])
        RB = c.sb("RB", [128, 8])
        OHBT = c.sb("OHBT", [128, 3 * 129])
        OHBR = c.sb("OHBR", [128, 3 * 128])
        RBC = c.sb("RBC", [128, 128])
        GT = c.sb("GT", [128, 384])
        EBF = c.sb("EBF", [128, 256])
        QROW = AA
        KN = BT
        VN = HH
        KC = [GX, XC]
        VCS = [RR, IG]
        LG = c.sb("LG", [128, 64])
        LG0 = c.sb("LG0", [128, 16])
        YA8 = c.sb("YA8", [128, 640])
        DMASK = c.sb("DMASK", [128, 512])
        c.op('pool', lambda e: e.memset(DMASK[:], 1.0), writes=[DMASK])
        c.op('pool', lambda e: e.affine_select(out=DMASK[:], in_=DMASK[:], pattern=[[1, 512]], compare_op=ALU.is_ge, fill=0.0, base=0, channel_multiplier=-64), reads=[DMASK], writes=[DMASK])
        c.op('pool', lambda e: e.affine_select(out=DMASK[:], in_=DMASK[:], pattern=[[-1, 512]], compare_op=ALU.is_ge, fill=0.0, base=63, channel_multiplier=64), reads=[DMASK], writes=[DMASK])
        c.dma('sp', RB[0:32, :], rel_bias, writes=[RB])
        c.dma('sp', OHBT[0:32, :].rearrange("p (r d) -> p r d", d=129), ohb.rearrange("r p d -> p r d"), writes=[OHBT])
        c.dma('sp', OHBR[0:32, :].rearrange("p (r d) -> p r d", d=128), ohbr.rearrange("r p d -> p r d"), writes=[OHBR])
        c.op('act', lambda e: e.activation(out=E0[0:1, :], in_=RB[0:1, :], func=AF.Exp), reads=[RB], writes=[E0])
        c.op('dve', lambda e: e.tensor_scalar(out=E0[0:1, :], in0=E0[0:1, :], scalar1=3.0, scalar2=None, op0=ALU.mult), reads=[E0], writes=[E0])
        OGC = c.sb("OGC", [128, 8])
        KEYT = c.sb("KEYT", [128, 2048], BF16)
        OBT = [c.sb(f"OBT{i}", [128, 520]) for i in range(3)]
        TV = c.sb("TV", [128, 256])
        TI = c.sb("TI", [128, 256], U32)
        TIF = c.sb("TIF", [128, 256])
        SW = c.sb("SW", [128, 128])
        SVT = c.sb("SVT", [128, 128])
        CI = c.sb("CI", [128, 128], U32)
        CWK = c.sb("CWK", [128, 256])
        GWP = [c.sb(f"GW{i}", [128, 128]) for i in range(2)]
        CJ = c.sb("CJ", [128, 256], U32)
        CJF = c.sb("CJF", [128, 256])
        EF_ = c.sb("EF_", [128, 256])
        EIDXP = [c.sb(f"EIDX{i}", [128, 128], I32) for i in range(2)]
        HPREP = [c.sb(f"HPRE{i}", [128, 128]) for i in range(2)]
        IOTA16 = c.sb("IOTA16", [128, 16])
        JUNKT = c.sb("JUNK", [128, 1024], BF16)
        JUNK = JUNKT
        VR1 = c.sb("VR1", [128, 1024], BF16)
        c.op('pool', lambda e: e.iota(IOTA16[:], pattern=[[1, 16]], base=0, channel_multiplier=0, allow_small_or_imprecise_dtypes=True), writes=[IOTA16])

        def build_ebm():
            c.op('pool', lambda e: e.memset(GT[:], 0.0), writes=[GT])
            for r in range(3):
                c.op('pe', lambda e: e.matmul(PB[1][:, 0:8], lhsT=OHBR[0:32, r * 128:(r + 1) * 128], rhs=RB[0:32, :], start=True, stop=True), reads=[OHBR, RB], writes=[PB[1]])
                c.op('act', lambda e: e.activation(out=EFS[:, r * 8:(r + 1) * 8], in_=PB[1][:, 0:8], func=AF.Exp), reads=[PB[1]], writes=[EFS])
                for h in range(8):
                    a = RB[0:32, h:h + 1]
                    c.op('dve', lambda e: e.tensor_copy(RBC[0:32, :], bass.AP(tensor=a.tensor, offset=a.offset, ap=[list(a.ap[0]), [0, 128]])), reads=[RB], writes=[RBC])
                    c.op('pe', lambda e: e.matmul(PB[0][:, 0:129], lhsT=RBC[0:32, :], rhs=OHBT[0:32, r * 129:(r + 1) * 129], start=True, stop=True), reads=[RBC, OHBT], writes=[PB[0]])
                    c.op('act', lambda e: e.activation(out=GT[:, 127:256], in_=PB[0][:, 0:129], func=AF.Exp), reads=[PB[0]], writes=[GT])
                    c.dma('sp', EB, GT[:], reads=[GT], writes=[tD["EB"]])
                    c.dma('sp', EBF[:, 0:128], bass.AP(tensor=EB.tensor, offset=EB.offset + 255, ap=[[383, 128], [1, 128]]), reads=[tD["EB"]], writes=[EBF])
                    c.dma('sp', EBF[:, 128:256], bass.AP(tensor=EB.tensor, offset=EB.offset + 127, ap=[[383, 128], [1, 128]]), reads=[tD["EB"]], writes=[EBF])
                    c.op('act', lambda e: e.copy(EBMv[:, (r * 8 + h) * 256:(r * 8 + h + 1) * 256], EBF[:]), reads=[EBF], writes=EBMd)


        def rstd_from_ss(ss_ap, out_ap, n, tile):
            p = ss_ap.shape[0]
            c.op('act', lambda e: e.activation(out=out_ap, in_=ss_ap, func=AF.Sqrt, scale=1.0 / n, bias=EPSB[0:p, 0:1]), reads=[tile, EPSB], writes=[tile])
            c.op('dve', lambda e: e.reciprocal(out=out_ap, in_=out_ap), reads=[tile], writes=[tile])

        def bc_heads(tile, c0, n, nh):
            a = tile[:, c0:c0 + n]
            return bass.AP(tensor=a.tensor, offset=a.offset, ap=[list(a.ap[0]), [0, nh], [1, n]])

        for l in range(DEPTH_RUN):
            xsrc = xin if l == 0 else XL
            xsrc_t = T(None) if l == 0 else tD["XL"]
            CS = TMP[0]
            c.dma('sp', CS[0:32, :], cc, writes=[CS])
            c.op('act', lambda e: e.activation(out=CS[0:32, :], in_=CS[0:32, :], func=AF.Silu), reads=[CS], writes=[CS])
            SILT = TMP[1]
            for k in range(8):
                c.op('pe', lambda e: e.transpose(PB[0][:, k * 32:(k + 1) * 32], CS[0:32, k * 128:(k + 1) * 128], ID[0:32, 0:32]), reads=[CS, ID], writes=[PB[0]])
            c.op('act', lambda e: e.copy(SILT[:, 0:256], PB[0][:, 0:256]), reads=[PB[0]], writes=[SILT])
            BRow = HB
            for ps_ in range(6):
                c.dma('sp', UV[0:1, :], b_ada[l:l + 1, ps_ * 1024:(ps_ + 1) * 1024], writes=[UV])
                for k in range(8):
                    st = STG[k % 2]
                    c.dma('sp', st[:], w_ada[l, k * 128:(k + 1) * 128, ps_ * 1024:(ps_ + 1) * 1024], writes=[st])
                    for j in range(2):
                        c.op('pe', lambda e: e.matmul(PB[j][0:32, :], lhsT=SILT[:, k * 32:(k + 1) * 32], rhs=st[:, j * 512:(j + 1) * 512], start=(k == 0), stop=False), reads=[SILT, st], writes=[PB[j]])
                for j in range(2):
                    col = ps_ * 1024 + j * 512
                    c.op('pe', lambda e: e.matmul(PB[j][0:32, :], lhsT=ONESR[0:1, 0:32], rhs=UV[0:1, j * 512:(j + 1) * 512], start=False, stop=True), reads=[ONESR, UV], writes=[PB[j]])
                    ev = EV[j % 2]
                    c.op('act' if j % 2 else 'dve', (lambda e: e.copy(ev[0:32, :], PB[j][0:32, :])) if j % 2 else (lambda e: e.tensor_copy(ev[0:32, :], PB[j][0:32, :])), reads=[PB[j]], writes=[ev])
                    c.dma('sp', ADA[l, :, col:col + 512], ev[0:32, :], reads=[ev], writes=[tD["ADA"]])

            def load_cast(dst_tile, dst_col, src_ap, ncols, scale_ap=None, scale_tile=None):
                cnt = load_cast.cnt
                for c0 in range(0, ncols, 1024):
                    n = min(1024, ncols - c0)
                    st = STG[cnt % 2]
                    c.dma('sp', st[:, 0:n], src_ap[:, c0:c0 + n], writes=[st])
                    dst = dst_tile[:, dst_col + c0:dst_col + c0 + n]
                    if scale_ap is not None:
                        c.op('dve', lambda e: e.tensor_scalar(out=dst, in0=st[:, 0:n], scalar1=scale_ap, scalar2=None, op0=ALU.mult), reads=[st, scale_tile], writes=[dst_tile])
                    elif cnt % 2:
                        c.op('act', lambda e: e.copy(dst, st[:, 0:n]), reads=[st], writes=[dst_tile])
                    else:
                        c.op('dve', lambda e: e.tensor_copy(dst, st[:, 0:n]), reads=[st], writes=[dst_tile])
                    cnt += 1
                load_cast.cnt = cnt
            load_cast.cnt = 0
            for k in range(8):
                load_cast(WB, k * DIN, w_in[l, k * 128:(k + 1) * 128, :], DIN)

            def load_mod(sample, stage):
                c0, n = (0, 2 * D) if stage == 1 else (2 * D, 4 * D)
                if sample:
                    c.dma('sp', MOD[0:16, 0:n], ADA[l, 1:17, c0:c0 + n], reads=[tD["ADA"]], writes=[MOD])
                else:
                    c.dma('sp', MOD[:, 0:n], ADA[l, 0, c0:c0 + n].partition_broadcast(128), reads=[tD["ADA"]], writes=[MOD])
                nsrc = norm_mix if stage == 1 else norm_ffn
                sc = (D, 2 * D) if stage == 1 else (2 * D, 3 * D)
                c.dma('sp', TMP[0][:], nsrc[l].partition_broadcast(128), writes=[TMP[0]])
                c.op('dve', lambda e: e.scalar_tensor_tensor(out=MOD[:, sc[0]:sc[1]], in0=MOD[:, sc[0]:sc[1]], scalar=1.0, in1=TMP[0][:], op0=ALU.add, op1=ALU.mult), reads=[MOD, TMP[0]], writes=[MOD])

            load_mod(False, 1)
            c.dma('sp', QG[:, 0:64], q_gain[l].partition_broadcast(128), writes=[QG])
            c.dma('sp', QG[:, 64:128], k_gain[l].partition_broadcast(128), writes=[QG])
            c.op('dve', lambda e: e.tensor_scalar(out=QG[:, 0:64], in0=QG[:, 0:64], scalar1=0.125, scalar2=None, op0=ALU.mult), reads=[QG], writes=[QG])
            c.dma('sp', GN[:], gmlp_norm[l].partition_broadcast(128), writes=[GN])
            c.dma('sp', BSC[:, 0:4], b_s[l].rearrange("g i -> i g"), writes=[BSC], allow_slow_non_contiguous=True)
            c.dma('sp', BSS[:, 0:4], w_s[l, :, 0, 0].partition_broadcast(128), writes=[BSS], allow_slow_non_contiguous=True)
            c.dma('sp', BSS[:, 4:8], b_s[l, :, 0].partition_broadcast(128), writes=[BSS], allow_slow_non_contiguous=True)
            for g in range(4):
                st = EV[g % 2]
                c.dma('sp', st[:, 0:128], w_s[l, g], writes=[st])
                c.op('pe', lambda e: e.transpose(PB[6][:, 0:128], st[:, 0:128], ID[:]), reads=[st, ID], writes=[PB[6]])
                c.op('dve', lambda e: e.tensor_tensor(out=WST[:, g * 128:(g + 1) * 128], in0=PB[6][:, 0:128], in1=TRIL[:], op=ALU.mult), reads=[PB[6], TRIL], writes=[WST])

            for t in range(NT if STAGES['s1'] else 0):
                sample = (t == NT - 1)
                if sample:
                    load_mod(True, 1)
                r0 = t * 128
                xt = XT[t % 2]
                c.dma('sp', xt[:], xsrc[r0:r0 + 128, :], reads=[xsrc_t], writes=[xt])
                c.op('act', lambda e: e.activation(out=TMP[0][:], in_=xt[:], func=AF.Square, accum_out=SM[:, 0:1]), reads=[xt], writes=[TMP[0], SM])
                rstd_from_ss(SM[:, 0:1], SM[:, 1:2], D, SM)
                c.op('dve', lambda e: e.scalar_tensor_tensor(out=TMP[1][:], in0=xt[:], scalar=SM[:, 1:2], in1=MOD[:, D:2 * D], op0=ALU.mult, op1=ALU.mult), reads=[xt, SM, MOD], writes=[TMP[1]])
                c.op('pool', lambda e: e.tensor_tensor(out=HB[:], in0=TMP[1][:], in1=MOD[:, 0:D], op=ALU.add), reads=[TMP[1], MOD], writes=[HB])
                for k in range(8):
                    c.op('pe', lambda e: e.transpose(PT[:, k * 128:(k + 1) * 128], HB[:, k * 128:(k + 1) * 128], IDB[:]), reads=[HB, IDB], writes=[PT])
                c.op('act', lambda e: e.copy(HT[:], PT[:]), reads=[PT], writes=[HT])
                for jb in range(5):
                    for k in range(8):
                        c.op('pe', lambda e: e.matmul(PB[jb][:], lhsT=HT[:, k * 128:(k + 1) * 128], rhs=WB[:, k * DIN + jb * 512:k * DIN + (jb + 1) * 512], start=(k == 0), stop=(k == 7)), reads=[HT, WB], writes=[PB[jb]])
                for cg in range(4):
                    for k in range(8):
                        c.op('pe', lambda e: e.matmul(PB[5][:, cg * 128:(cg + 1) * 128], lhsT=WB[:, k * DIN + 2048 + cg * 128:k * DIN + 2048 + (cg + 1) * 128], rhs=HT[:, k * 128:(k + 1) * 128], start=(k == 0), stop=(k == 7)), reads=[HT, WB], writes=[PB[5]])
                for qi in range(2):
                    pb = PB[qi]
                    so = 8 + 16 * qi
                    c.op('act', lambda e: e.activation(out=TMP[0][:, 0:512], in_=pb[:], func=AF.Square), reads=[pb], writes=[TMP[0]])
                    c.op('dve', lambda e: e.tensor_reduce(out=SM[:, so:so + 8], in_=TMP[0][:, 0:512].rearrange("p (h e) -> p h e", e=64), axis=AX.X, op=ALU.add), reads=[TMP[0]], writes=[SM])
                    rstd_from_ss(SM[:, so:so + 8], SM[:, so + 8:so + 16], 64, SM)
                    qv = QKV[:, qi * 512:(qi + 1) * 512].rearrange("p (h e) -> p h e", e=64)
                    c.op('dve', lambda e: e.tensor_tensor(out=qv, in0=pb[:].rearrange("p (h e) -> p h e", e=64), in1=bcl(SM[:, so + 8:so + 16], 64), op=ALU.mult), reads=[pb, SM], writes=[QKV])
                    c.op('pool', lambda e: e.tensor_tensor(out=qv, in0=qv, in1=bc_heads(QG, qi * 64, 64, 8), op=ALU.mult), reads=[QKV, QG], writes=[QKV])
                c.op('act', lambda e: e.copy(QKB[:], QKV[:, 0:1024]), reads=[QKV], writes=[QKB])
                c.op('act', lambda e: e.copy(QKV[:, 1024:1536], PB[2][:]), reads=[PB[2]], writes=[QKV])
                c.op('dve', lambda e: e.tensor_copy(VB[:].rearrange("p (h e) -> p h e", e=65)[:, :, 0:64], PB[2][:].rearrange("p (h e) -> p h e", e=64)), reads=[PB[2]], writes=[VB])
                if not sample:
                    if t >= 16:
                        c.dma('sp', k_p[l, (t - 16) * 128:(t - 15) * 128, :], QKV[:, 512:1024], reads=[QKV], writes=[tD["OUT"]])
                        c.dma('sp', v_p[l, (t - 16) * 128:(t - 15) * 128, :], QKV[:, 1024:1536], reads=[QKV], writes=[tD["OUT"]])
                    for j in range(8):
                        c.op('pe', lambda e: e.transpose(PT[:, j * 128:(j + 1) * 128], QKB[:, j * 128:(j + 1) * 128], IDB[:]), reads=[QKB, IDB], writes=[PT])
                    c.op('dve', lambda e: e.tensor_copy(QKT[:], PT[:]), reads=[PT], writes=[QKT])
                    c.dma('sp', QT[:, :, r0:r0 + 128].rearrange("h p t -> p h t"), QKT[:, 0:512].rearrange("p (h t) -> p h t", t=128), reads=[QKT], writes=[tD["QT"]])
                    c.dma('sp', KT[:, :, r0:r0 + 128].rearrange("h p t -> p h t"), QKT[:, 512:1024].rearrange("p (h t) -> p h t", t=128), reads=[QKT], writes=[tD["KT"]])
                    c.dma('sp', VS[r0:r0 + 128, :], VB[:], reads=[VB], writes=[tD["VS"]])
                else:
                    c.dma('sp', k_s[l], QKV[0:16, 512:1024], reads=[QKV], writes=[tD["OUT"]])
                    c.dma('sp', v_s[l], QKV[0:16, 1024:1536], reads=[QKV], writes=[tD["OUT"]])
                    c.dma('sp', SQKV.rearrange("a s d -> s a d"), QKV[0:16, 0:1536].rearrange("p (a d) -> p a d", d=512), reads=[QKV], writes=[tD["SQKV"]])
                c.op('act', lambda e: e.copy(UV[:, 0:512], PB[3][:]), reads=[PB[3]], writes=[UV])
                c.op('act', lambda e: e.activation(out=TMP[0][:, 0:256], in_=UV[:, 256:512], func=AF.Square, accum_out=SM[:, 2:3]), reads=[UV], writes=[TMP[0], SM])
                rstd_from_ss(SM[:, 2:3], SM[:, 3:4], 256, SM)
                c.op('dve', lambda e: e.scalar_tensor_tensor(out=UV[:, 256:512], in0=UV[:, 256:512], scalar=SM[:, 3:4], in1=GN[:], op0=ALU.mult, op1=ALU.mult), reads=[UV, SM, GN], writes=[UV])
                if t == NT - 2:
                    c.dma('sp', gv_p[l], UV[:, 256:512], reads=[UV], writes=[tD["OUT"]])
                if sample:
                    c.dma('sp', gv_s[l], UV[0:16, 256:512], reads=[UV], writes=[tD["OUT"]])
                    for g in range(4):
                        c.op('dve', lambda e: e.tensor_scalar(out=YBT[:, g * 64:(g + 1) * 64], in0=UV[:, 256 + g * 64:256 + (g + 1) * 64], scalar1=BSS[:, g:g + 1], scalar2=BSS[:, 4 + g:5 + g], op0=ALU.mult, op1=ALU.add), reads=[UV, BSS], writes=[YBT])
                    c.op('pool', lambda e: e.tensor_tensor(out=YBT[:], in0=YBT[:], in1=UV[:, 0:256], op=ALU.mult), reads=[YBT, UV], writes=[YBT])
                else:
                    for g in range(4):
                        c.op('pe', lambda e: e.matmul(PB[6][:, g * 64:(g + 1) * 64], lhsT=WST[:, g * 128:(g + 1) * 128], rhs=UV[:, 256 + g * 64:256 + (g + 1) * 64], start=True, stop=True), reads=[WST, UV], writes=[PB[6]])
                    for g in range(4):
                        c.op('dve', lambda e: e.scalar_tensor_tensor(out=YBT[:, g * 64:(g + 1) * 64], in0=PB[6][:, g * 64:(g + 1) * 64], scalar=BSC[:, g:g + 1], in1=UV[:, g * 64:(g + 1) * 64], op0=ALU.add, op1=ALU.mult), reads=[PB[6], BSC, UV], writes=[YBT])
                c.dma('sp', YB[r0:r0 + 128, :], YBT[:], reads=[YBT], writes=[tD["YB"]])
                if t >= NT - 2:
                    c.op('act', lambda e: e.copy(UV[:, 512:1024], PB[4][:]), reads=[PB[4]], writes=[UV])
                    if sample:
                        c.dma('sp', conv_s[l, :, 2, :], UV[0:16, 512:768], reads=[UV], writes=[tD["OUT"]])
                        c.dma('sp', conv_s[l, :, 0:2, :], state_conv[l, :, 256:768].rearrange("s (k c) -> s k c", c=256), writes=[tD["OUT"]])
                    else:
                        c.dma('sp', conv_p[l], UV[125:128, 512:768], reads=[UV], writes=[tD["OUT"]])
                c.op('dve', lambda e: e.tensor_copy(FT[:, 0:256], PB[5][:, 0:256]), reads=[PB[5]], writes=[FT])
                c.op('act', lambda e: e.activation(out=FT[:, 256:512], in_=PB[5][:, 256:512], func=AF.Gelu), reads=[PB[5]], writes=[FT])
                c.dma('sp', XRT[:, :, r0:r0 + 128].rearrange("c p t -> p c t"), FT[:, 0:256].rearrange("p (c t) -> p c t", t=128), reads=[FT], writes=[tD["XRT"]])
                c.dma('sp', GXT[:, :, r0:r0 + 128].rearrange("c p t -> p c t"), FT[:, 256:512].rearrange("p (c t) -> p c t", t=128), reads=[FT], writes=[tD["GXT"]])

            if STAGES['lru']:
                for cch_ in range(2):
                    for k_ in range(4):
                        c.dma('sp', CW[:, cch_ * 4 + k_:cch_ * 4 + k_ + 1], conv_w[l, k_, cch_ * 128:(cch_ + 1) * 128].rearrange("(p o) -> p o", o=1), writes=[CW], allow_slow_non_contiguous=True)
                c.dma('sp', CW[:, 8:10], conv_b[l].rearrange("(c p) -> p c", p=128), writes=[CW], allow_slow_non_contiguous=True)
                c.dma('sp', BAX[:, 0:2], b_a[l].rearrange("(c p) -> p c", p=128), writes=[BAX], allow_slow_non_contiguous=True)
                c.dma('sp', BAX[:, 2:4], b_x[l].rearrange("(c p) -> p c", p=128), writes=[BAX], allow_slow_non_contiguous=True)
                c.dma('sp', LAM[:, 0:2], lru_lambda[l].rearrange("(c p) -> p c", p=128), writes=[LAM], allow_slow_non_contiguous=True)
                c.op('act', lambda e: e.activation(out=LAM[:, 2:4], in_=LAM[:, 0:2], func=AF.Exp, scale=-1.0), reads=[LAM], writes=[LAM])
                c.op('act', lambda e: e.activation(out=LAM[:, 2:4], in_=LAM[:, 2:4], func=AF.Ln, bias=ONES[:, 0:1]), reads=[LAM, ONES], writes=[LAM])
                c.op('dve', lambda e: e.tensor_scalar(out=LAM[:, 4:6], in0=LAM[:, 2:4], scalar1=-16.0, scalar2=None, op0=ALU.mult), reads=[LAM], writes=[LAM])
                c.op('dve', lambda e: e.tensor_scalar(out=LAM[:, 2:4], in0=LAM[:, 2:4], scalar1=-8.0, scalar2=None, op0=ALU.mult), reads=[LAM], writes=[LAM])
                c.op('pool', lambda e: e.memset(WAB[:], 0.0), writes=[WAB])
                for gi, wsrc in enumerate((w_a, w_x)):
                    for h in range(4):
                        cch, hh = h // 2, h % 2
                        base = (gi * 2 + cch) * 128
                        c.dma('sp', WAB[hh * 64:(hh + 1) * 64, base + hh * 64:base + (hh + 1) * 64], wsrc[l, h], writes=[WAB])
                c.op('pool', lambda e: e.memset(SST[:], 0.0), writes=[SST])
                c.dma('sp', SST[0:16, 0:768], state_conv[l], writes=[SST])
                c.dma('sp', SST[0:16, 768:1024], state_h[l], writes=[SST])
                for j in range(8):
                    pb = PB[j % 4]
                    c.op('pe', lambda e: e.transpose(pb[:, 0:128], SST[:, j * 128:(j + 1) * 128], ID[:]), reads=[SST, ID], writes=[pb])
                    c.op('act', lambda e: e.copy(SCT[:, j * 128:(j + 1) * 128], pb[:, 0:128]), reads=[pb], writes=[SCT])
                for cch in range(2):
                    w = lambda k: CW[:, cch * 4 + k:cch * 4 + k + 1]
                    c.op('pool', lambda e: e.memset(XRC[:, 0:3], 0.0), writes=[XRC])
                    c.op('pool', lambda e: e.memset(HPREV[:, cch:cch + 1], 0.0), writes=[HPREV])
                    for tc in range(9):
                        smp = (tc == 8)
                        n = 128 if smp else 512
                        c0 = tc * 512
                        if smp:
                            c.dma('sp', XRC[:, 3:3 + n], XRT[cch, :, c0:c0 + n], reads=[tD["XRT"]], writes=[XRC])
                        else:
                            c.dma('sp', XRC[:, 3:515], XRT[cch, :, c0:c0 + 512], reads=[tD["XRT"]], writes=[XRC])
                        c.dma('sp', GX[:, 0:n], GXT[cch, :, c0:c0 + n], reads=[tD["GXT"]], writes=[GX])
                        c.op('dve', lambda e: e.tensor_scalar(out=XC[:, 0:n], in0=XRC[:, 3:3 + n], scalar1=w(3), scalar2=CW[:, 8 + cch:9 + cch], op0=ALU.mult, op1=ALU.add), reads=[XRC, CW], writes=[XC])
                        for k in range(3):
                            if smp:
                                src = SCT[:, (k * 2 + cch) * 128:(k * 2 + cch + 1) * 128]
                                rd = [SCT]
                            else:
                                src = XRC[:, k:k + n]
                                rd = [XRC]
                            c.op('dve', lambda e: e.scalar_tensor_tensor(out=XC[:, 0:n], in0=src, scalar=w(k), in1=XC[:, 0:n], op0=ALU.mult, op1=ALU.add), reads=rd + [CW, XC], writes=[XC])
                        c.op('pe', lambda e: e.matmul(PB[0][:, 0:n], lhsT=WAB[:, cch * 128:(cch + 1) * 128], rhs=XC[:, 0:n], start=True, stop=True), reads=[WAB, XC], writes=[PB[0]])
                        c.op('pe', lambda e: e.matmul(PB[1][:, 0:n], lhsT=WAB[:, (2 + cch) * 128:(3 + cch) * 128], rhs=XC[:, 0:n], start=True, stop=True), reads=[WAB, XC], writes=[PB[1]])
                        c.op('act', lambda e: e.activation(out=RR[:, 0:n], in_=PB[0][:, 0:n], func=AF.Sigmoid, bias=BAX[:, cch:cch + 1]), reads=[PB[0], BAX], writes=[RR])
                        c.op('act', lambda e: e.activation(out=IG[:, 0:n], in_=PB[1][:, 0:n], func=AF.Sigmoid, bias=BAX[:, 2 + cch:3 + cch]), reads=[PB[1], BAX], writes=[IG])
                        c.op('act', lambda e: e.activation(out=AA[:, 0:n], in_=RR[:, 0:n], func=AF.Exp, scale=LAM[:, 2 + cch:3 + cch]), reads=[RR, LAM], writes=[AA])
                        c.op('act', lambda e: e.activation(out=BT[:, 0:n], in_=RR[:, 0:n], func=AF.Exp, scale=LAM[:, 4 + cch:5 + cch]), reads=[RR, LAM], writes=[BT])
                        c.op('dve', lambda e: e.tensor_scalar(out=BT[:, 0:n], in0=BT[:, 0:n], scalar1=-1.0, scalar2=1.0, op0=ALU.mult, op1=ALU.add), reads=[BT], writes=[BT])
                        c.op('act', lambda e: e.activation(out=BT[:, 0:n], in_=BT[:, 0:n], func=AF.Sqrt), reads=[BT], writes=[BT])
                        c.op('dve', lambda e: e.tensor_tensor(out=BT[:, 0:n], in0=BT[:, 0:n], in1=IG[:, 0:n], op=ALU.mult), reads=[BT, IG], writes=[BT])
                        c.op('pool', lambda e: e.tensor_tensor(out=BT[:, 0:n], in0=BT[:, 0:n], in1=XC[:, 0:n], op=ALU.mult), reads=[BT, XC], writes=[BT])
                        if smp:
                            h0 = SCT[:, (6 + cch) * 128:(7 + cch) * 128]
                            c.op('dve', lambda e: e.tensor_tensor(out=HH[:, 0:n], in0=AA[:, 0:n], in1=h0, op=ALU.mult), reads=[AA, SCT], writes=[HH])
                            c.op('dve', lambda e: e.tensor_tensor(out=HH[:, 0:n], in0=HH[:, 0:n], in1=BT[:, 0:n], op=ALU.add), reads=[HH, BT], writes=[HH])
                            c.dma('sp', h_s[l, :, cch * 128:(cch + 1) * 128].rearrange("s c -> c s"), HH[:, 0:16], reads=[HH], writes=[tD["OUT"]], allow_slow_non_contiguous=True)
                        else:
                            c.op('dve', lambda e: e.tensor_tensor_scan(out=HH[:, 0:n], data0=AA[:, 0:n], data1=BT[:, 0:n], initial=HPREV[:, cch:cch + 1], op0=ALU.mult, op1=ALU.add), reads=[AA, BT, HPREV], writes=[HH])
                            c.op('act', lambda e: e.copy(HPREV[:, cch:cch + 1], HH[:, 511:512]), reads=[HH], writes=[HPREV])
                            c.op('pool', lambda e: e.tensor_copy(XRC[:, 0:3], XRC[:, 512:515]), reads=[XRC], writes=[XRC])
                            if tc == 7:
                                c.dma('sp', h_p[l, cch * 128:(cch + 1) * 128].rearrange("(c o) -> c o", o=1), HH[:, 511:512], reads=[HH], writes=[tD["OUT"]], allow_slow_non_contiguous=True)
                        c.op('dve', lambda e: e.tensor_tensor(out=YCB[:, 0:n], in0=GX[:, 0:n], in1=HH[:, 0:n], op=ALU.mult), reads=[GX, HH], writes=[YCB])
                        c.dma('sp', YCT[cch, :, c0:c0 + n], YCB[:, 0:n], reads=[YCB], writes=[tD["YCT"]])

            if STAGES['attn']:
                build_ebm()
                for hp in range(4):
                    c.dma('sp', QTSv, QT[hp], reads=[tD["QT"]], writes=QTSd)
                    c.dma('sp', KTSv, KT[hp], reads=[tD["KT"]], writes=KTSd)
                    blk = 0
                    for r, dil in enumerate(DILS):
                        nb = SEQ // dil // 128
                        for p in range(dil):
                            for n in range(nb):
                                base = p + dil * n * 128
                                vc = VC[n % 2]
                                vp = VC[(n + 1) % 2]
                                c.dma('sp', vc[:], bass.AP(tensor=VS.tensor, offset=VS.offset + base * 520 + hp * 130, ap=[[520 * dil, 128], [1, 130]]), reads=[tD["VS"]], writes=[vc])
                                po = PB[2 + blk % 2]
                                for hh in range(2):
                                    h = hp * 2 + hh
                                    ps0 = hh * 64
                                    pst = PB[hh]

                                    def tk(tile, b0):
                                        a = tile[ps0:ps0 + 64, b0:b0 + 1]
                                        return bass.AP(tensor=a.tensor, offset=a.offset, ap=[list(a.ap[0]), [dil, 128]])
                                    qa = tk(QTSv, base)
                                    c.op('pe', lambda e: e.matmul(pst[:, 128:256], lhsT=tk(KTSv, base), rhs=qa, start=True, stop=True), reads=KTSd + QTSd, writes=[pst])
                                    if n > 0:
                                        c.op('pe', lambda e: e.matmul(pst[:, 0:128], lhsT=tk(KTSv, base - dil * 128), rhs=qa, start=True, stop=True), reads=KTSd + QTSd, writes=[pst])
                                    lo = 0 if n > 0 else 128
                                    pe_ = PE_[hh]
                                    pm = PM[hh]
                                    c.op('act', lambda e: e.activation(out=pe_[:, lo:256], in_=pst[:, lo:256], func=AF.Exp), reads=[pst], writes=[pe_])
                                    eb = EBMv[:, (r * 8 + h) * 256 + lo:(r * 8 + h) * 256 + 256]
                                    c.op('dve' if hh == 0 else 'pool', lambda e: e.tensor_tensor(out=pm[:, lo:256], in0=pe_[:, lo:256], in1=eb, op=ALU.mult), reads=[pe_] + EBMd, writes=[pm])
                                    c.op('pe', lambda e: e.matmul(po[:, hh * 65:(hh + 1) * 65], lhsT=pm[:, 128:256], rhs=vc[:, hh * 65:(hh + 1) * 65], start=True, stop=(n == 0)), reads=[pm, vc], writes=[po])
                                    if n > 0:
                                        c.op('pe', lambda e: e.matmul(po[:, hh * 65:(hh + 1) * 65], lhsT=pm[:, 0:128], rhs=vp[:, hh * 65:(hh + 1) * 65], start=False, stop=True), reads=[pm, vp], writes=[po])
                                os_ = OS[blk % 2]
                                c.op('act', lambda e: e.copy(os_[:], po[:, 0:130]), reads=[po], writes=[os_])
                                c.dma('sp', bass.AP(tensor=OB.tensor, offset=OB.offset + r * SEQ * 520 + base * 520 + hp * 130, ap=[[520 * dil, 128], [1, 130]]), os_[:], reads=[os_], writes=[tD["OB"]])
                                blk += 1
                for s in range(NS):
                    c.dma('sp', QROW[:], SQKV[0, s].partition_broadcast(128), reads=[tD["SQKV"]], writes=[QROW])
                    c.dma('sp', KN[0:1, :], SQKV[1, s:s + 1, :], reads=[tD["SQKV"]], writes=[KN])
                    c.dma('sp', VN[0:1, :], SQKV[2, s:s + 1, :], reads=[tD["SQKV"]], writes=[VN])
                    for r, dil in enumerate(DILS):
                        kc = KC[r % 2]
                        vcs = VCS[r % 2]
                        row0 = 2048 - 128 * dil
                        c.dma('sp', kc[:], bass.AP(tensor=cache_k.tensor, offset=cache_k.offset + ((l * NS + s) * 2048 + row0) * 512, ap=[[512 * dil, 128], [1, 512]]), writes=[kc])
                        c.dma('sp', vcs[:], bass.AP(tensor=cache_v.tensor, offset=cache_v.offset + ((l * NS + s) * 2048 + row0) * 512, ap=[[512 * dil, 128], [1, 512]]), writes=[vcs])
                        c.op('dve', lambda e: e.tensor_tensor(out=kc[:], in0=kc[:], in1=QROW[:], op=ALU.mult), reads=[kc, QROW], writes=[kc])
                        c.op('dve', lambda e: e.tensor_reduce(out=LG[:, r * 8:(r + 1) * 8], in_=kc[:].rearrange("p (h e) -> p h e", e=64), axis=AX.X, op=ALU.add), reads=[kc], writes=[LG])
                        c.op('act', lambda e: e.activation(out=LG[:, 32 + r * 8:32 + (r + 1) * 8], in_=LG[:, r * 8:(r + 1) * 8], func=AF.Exp), reads=[LG], writes=[LG])
                        c.op('dve', lambda e: e.tensor_tensor(out=LG[:, 32 + r * 8:32 + (r + 1) * 8], in0=LG[:, 32 + r * 8:32 + (r + 1) * 8], in1=EFS[:, r * 8:(r + 1) * 8], op=ALU.mult), reads=[LG, EFS], writes=[LG])
                        c.op('pe', lambda e: e.matmul(PB[4][0:8, :], lhsT=LG[:, 32 + r * 8:32 + (r + 1) * 8], rhs=vcs[:], start=(r == 0), stop=False), reads=[LG, vcs], writes=[PB[4]])
                        c.op('pe', lambda e: e.matmul(PB[5][0:8, 0:1], lhsT=LG[:, 32 + r * 8:32 + (r + 1) * 8], rhs=ONES[:, 0:1], start=(r == 0), stop=False), reads=[LG, ONES], writes=[PB[5]])
                    c.op('dve', lambda e: e.tensor_tensor(out=KN[0:1, :], in0=KN[0:1, :], in1=QROW[0:1, :], op=ALU.mult), reads=[KN, QROW], writes=[KN])
                    c.op('dve', lambda e: e.tensor_reduce(out=LG0[0:1, 0:8], in_=KN[0:1, :].rearrange("p (h e) -> p h e", e=64), axis=AX.X, op=ALU.add), reads=[KN], writes=[LG0])
                    c.op('act', lambda e: e.activation(out=LG0[0:1, 8:16], in_=LG0[0:1, 0:8], func=AF.Exp), reads=[LG0], writes=[LG0])
                    c.op('dve', lambda e: e.tensor_tensor(out=LG0[0:1, 8:16], in0=LG0[0:1, 8:16], in1=E0[0:1, :], op=ALU.mult), reads=[LG0, E0], writes=[LG0])
                    c.op('pe', lambda e: e.matmul(PB[4][0:8, :], lhsT=LG0[0:1, 8:16], rhs=VN[0:1, :], start=False, stop=True), reads=[LG0, VN], writes=[PB[4]])
                    c.op('pe', lambda e: e.matmul(PB[5][0:8, 0:1], lhsT=LG0[0:1, 8:16], rhs=ONES[0:1, 0:1], start=False, stop=True), reads=[LG0, ONES], writes=[PB[5]])
                    c.op('dve', lambda e: e.tensor_tensor(out=YA8[0:8, 0:512], in0=PB[4][0:8, :], in1=DMASK[0:8, :], op=ALU.mult), reads=[PB[4], DMASK], writes=[YA8])
                    c.op('dve', lambda e: e.tensor_reduce(out=YA8[0:8, 512:576], in_=YA8[0:8, 0:512].rearrange("p (h e) -> p e h", e=64), axis=AX.X, op=ALU.add), reads=[YA8], writes=[YA8])
                    c.op('dve', lambda e: e.reciprocal(out=YA8[0:8, 576:577], in_=PB[5][0:8, 0:1]), reads=[PB[5]], writes=[YA8])
                    c.op('dve', lambda e: e.tensor_scalar(out=YA8[0:8, 512:576], in0=YA8[0:8, 512:576], scalar1=YA8[0:8, 576:577], scalar2=None, op0=ALU.mult), reads=[YA8], writes=[YA8])
                    c.dma('sp', YAS[s].rearrange("(h e) -> h e", e=64), YA8[0:8, 512:576], reads=[YA8], writes=[tD["YAS"]])

            if STAGES['s3']:
                last = (l == DEPTH_RUN - 1)
                xdst = y_out if last else XL
                xdst_t = tD["OUT"] if last else tD["XL"]
                c.dma('sp', OGC[:], out_gain[l].rearrange("(k p) -> p k", p=128), writes=[OGC], allow_slow_non_contiguous=True)
                for k in range(8):
                    load_cast(WB, k * 1024, w_out[l, k * 128:(k + 1) * 128, :], 1024, scale_ap=OGC[:, k:k + 1], scale_tile=OGC)
                for k in range(8):
                    load_cast(WB, 8192 + k * 2048, peer_wq[l, k * 128:(k + 1) * 128, :], 2048)
                for hp2 in range(16):
                    st = EV[hp2 % 2]
                    c.dma('sp', st[:, 0:128], peer_keys[l, hp2], writes=[st])
                    pb = PB[hp2 % 2]
                    c.op('pe', lambda e: e.transpose(pb[:, 0:128], st[:, 0:128], ID[:]), reads=[st, ID], writes=[pb])
                    c.op('act', lambda e: e.copy(KEYT[:, hp2 * 128:(hp2 + 1) * 128], pb[:, 0:128]), reads=[pb], writes=[KEYT])
                load_mod(False, 3)
                c.op('pool', lambda e: e.memset(S2[:, 0:768], 0.0), writes=[S2])
                H2P = [(LB[:, p_ * 1024:(p_ + 1) * 1024], [LBT[2 * p_], LBT[2 * p_ + 1]]) for p_ in range(2)]
                X1P = [(LB[:, 2048 + p_ * 1024:2048 + (p_ + 1) * 1024], [LBT[4 + 2 * p_], LBT[5 + 2 * p_]]) for p_ in range(2)]

                def front(t, p):
                    sample = (t == NT - 1)
                    if sample:
                        load_mod(True, 3)
                    r0 = t * 128
                    xt = XT[t % 2]
                    H2, H2d = H2P[p]
                    X1T, X1d = X1P[p]
                    EIDX, GW = EIDXP[p], GWP[p]
                    YAYB = S2
                    c.dma('sp', xt[:], xsrc[r0:r0 + 128, :], reads=[xsrc_t], writes=[xt])
                    if sample:
                        c.op('pool', lambda e: e.memset(YAYB[:, 0:512], 0.0), writes=[S2])
                        c.dma('sp', YAYB[0:16, 0:512], YAS, reads=[tD["YAS"]], writes=[S2])
                    else:
                        for r in range(3):
                            c.dma('sp', OBT[r][:], OB[r, r0:r0 + 128, :], reads=[tD["OB"]], writes=[OBT[r]])
                        c.op('dve', lambda e: e.tensor_tensor(out=OBT[0][:], in0=OBT[0][:], in1=OBT[1][:], op=ALU.add), reads=[OBT[0], OBT[1]], writes=[OBT[0]])
                        c.op('dve', lambda e: e.tensor_tensor(out=OBT[0][:], in0=OBT[0][:], in1=OBT[2][:], op=ALU.add), reads=[OBT[0], OBT[2]], writes=[OBT[0]])
                        o3 = OBT[0][:].rearrange("p (h e) -> p h e", e=65)
                        c.op('dve', lambda e: e.reciprocal(out=SM[:, 40:48], in_=o3[:, :, 64]), reads=[OBT[0]], writes=[SM])
                        c.op('dve', lambda e: e.tensor_tensor(out=YAYB[:, 0:512].rearrange("p (h e) -> p h e", e=64), in0=o3[:, :, 0:64], in1=bcl(SM[:, 40:48], 64), op=ALU.mult), reads=[OBT[0], SM], writes=[S2])
                    yield
                    c.dma('sp', YAYB[:, 512:768], YB[r0:r0 + 128, :], reads=[tD["YB"]], writes=[S2])
                    c.op('act', lambda e: e.activation(out=TMP[0][:, 0:512], in_=YAYB[:, 0:512], func=AF.Square, accum_out=SM[:, 4:5]), reads=[S2], writes=[TMP[0], SM])
                    rstd_from_ss(SM[:, 4:5], SM[:, 5:6], 512, SM)
                    c.op('act', lambda e: e.activation(out=TMP[0][:, 0:256], in_=YAYB[:, 512:768], func=AF.Square, accum_out=SM[:, 6:7]), reads=[S2], writes=[TMP[0], SM])
                    rstd_from_ss(SM[:, 6:7], SM[:, 7:8], 256, SM)
                    yield
                    MT = QKB
                    c.op('act', lambda e: e.copy(HB[:, 0:768], YAYB[:, 0:768]), reads=[S2], writes=[HB])
                    for k in range(6):
                        c.op('pe', lambda e: e.transpose(PT[:, k * 128:(k + 1) * 128], HB[:, k * 128:(k + 1) * 128], IDB[:]), reads=[HB, IDB], writes=[PT])
                    c.op('dve', lambda e: e.tensor_copy(MT[:, 0:768], PT[:, 0:768]), reads=[PT], writes=[MT])
                    yield
                    YCt = QKT
                    c.dma('sp', YCt[:, 0:256].rearrange("p (c t) -> p c t", t=128), YCT[:, :, r0:r0 + 128].rearrange("c p t -> p c t"), reads=[tD["YCT"]], writes=[QKT])
                    c.op('dve', lambda e: e.tensor_tensor(out=FT[:, 0:256], in0=YCt[:, 0:256], in1=YCt[:, 0:256], op=ALU.mult), reads=[QKT], writes=[FT])
                    for cch in range(2):
                        c.op('pe', lambda e: e.matmul(PB[3][:, 0:1], lhsT=FT[:, cch * 128:(cch + 1) * 128], rhs=ONES[:, 0:1], start=(cch == 0), stop=(cch == 1)), reads=[FT, ONES], writes=[PB[3]])
                    c.op('act', lambda e: e.copy(SM[:, 48:49], PB[3][:, 0:1]), reads=[PB[3]], writes=[SM])
                    rstd_from_ss(SM[:, 48:49], SM[:, 49:50], 256, SM)
                    yield
                    tt = TMP[0]
                    for half in range(2):
                        hs = slice(half * 512, (half + 1) * 512)
                        for k in range(4):
                            c.op('pe', lambda e: e.matmul(PB[0][:], lhsT=MT[:, k * 128:(k + 1) * 128], rhs=WB[:, k * 1024 + half * 512:k * 1024 + (half + 1) * 512], start=(k == 0), stop=(k == 3)), reads=[MT, WB], writes=[PB[0]])
                        for k in range(4, 6):
                            c.op('pe', lambda e: e.matmul(PB[1][:], lhsT=MT[:, k * 128:(k + 1) * 128], rhs=WB[:, k * 1024 + half * 512:k * 1024 + (half + 1) * 512], start=(k == 4), stop=(k == 5)), reads=[MT, WB], writes=[PB[1]])
                        for k in range(6, 8):
                            c.op('pe', lambda e: e.matmul(PB[2][:], lhsT=YCt[:, (k - 6) * 128:(k - 5) * 128], rhs=WB[:, k * 1024 + half * 512:k * 1024 + (half + 1) * 512], start=(k == 6), stop=(k == 7)), reads=[QKT, WB], writes=[PB[2]])
                        yield
                        c.op('dve', lambda e: e.tensor_scalar(out=tt[:, hs], in0=PB[0][:], scalar1=SM[:, 5:6], scalar2=None, op0=ALU.mult), reads=[PB[0], SM], writes=[tt])
                        c.op('dve', lambda e: e.scalar_tensor_tensor(out=tt[:, hs], in0=PB[1][:], scalar=SM[:, 7:8], in1=tt[:, hs], op0=ALU.mult, op1=ALU.add), reads=[PB[1], SM, tt], writes=[tt])
                        c.op('dve', lambda e: e.scalar_tensor_tensor(out=tt[:, hs], in0=PB[2][:], scalar=SM[:, 49:50], in1=tt[:, hs], op0=ALU.mult, op1=ALU.add), reads=[PB[2], SM, tt], writes=[tt])
                        yield
                    c.op('dve', lambda e: e.tensor_tensor(out=TMP[0][:], in0=TMP[0][:], in1=MOD[:, 0:D], op=ALU.mult), reads=[TMP[0], MOD], writes=[TMP[0]])
                    c.op('dve', lambda e: e.tensor_tensor(out=X1T, in0=TMP[0][:], in1=xt[:], op=ALU.add), reads=[TMP[0], xt], writes=X1d)
                    yield
                    c.op('act', lambda e: e.activation(out=TMP[0][:], in_=X1T, func=AF.Square, accum_out=SM[:, 50:51]), reads=X1d, writes=[TMP[0], SM])
                    rstd_from_ss(SM[:, 50:51], SM[:, 51:52], D, SM)
                    c.op('dve', lambda e: e.scalar_tensor_tensor(out=H2, in0=X1T, scalar=SM[:, 51:52], in1=MOD[:, 2 * D:3 * D], op0=ALU.mult, op1=ALU.mult), reads=X1d + [SM, MOD], writes=H2d)
                    c.op('dve', lambda e: e.tensor_tensor(out=H2, in0=H2, in1=MOD[:, D:2 * D], op=ALU.add), reads=H2d + [MOD], writes=H2d)
                    yield
                    c.op('act', lambda e: e.copy(HB[:], H2), reads=H2d, writes=[HB])
                    for k in range(8):
                        c.op('pe', lambda e: e.transpose(PT[:, k * 128:(k + 1) * 128], HB[:, k * 128:(k + 1) * 128], IDB[:]), reads=[HB, IDB], writes=[PT])
                    c.op('act', lambda e: e.copy(HT[:], PT[:]), reads=[PT], writes=[HT])
                    yield
                    QTB = QKT
                    for g4 in range(4):
                        sb_ = PB[1 + g4 % 2]
                        for i in range(4):
                            hp2 = g4 * 4 + i
                            for k in range(8):
                                c.op('pe', lambda e: e.matmul(PB[0][:, i * 128:(i + 1) * 128], lhsT=WB[:, 8192 + k * 2048 + hp2 * 128:8192 + k * 2048 + (hp2 + 1) * 128], rhs=HT[:, k * 128:(k + 1) * 128], start=(k == 0), stop=(k == 7)), reads=[WB, HT], writes=[PB[0]])
                            yield
                        c.op('act', lambda e: e.copy(QTB[:, 0:512], PB[0][:]), reads=[PB[0]], writes=[QKT])
                        c.op('act', lambda e: e.activation(out=FT[:], in_=PB[0][:], func=AF.Square), reads=[PB[0]], writes=[FT])
                        for i in range(4):
                            hp2 = g4 * 4 + i
                            c.op('pe', lambda e: e.matmul(PB[3][:, hp2:hp2 + 1], lhsT=FT[:, i * 128:(i + 1) * 128], rhs=ONES[:, 0:1], start=True, stop=True), reads=[FT, ONES], writes=[PB[3]])
                            c.op('pe', lambda e: e.matmul(sb_[:, i * 128:(i + 1) * 128], lhsT=QTB[:, i * 128:(i + 1) * 128], rhs=KEYT[:, hp2 * 128:(hp2 + 1) * 128], start=True, stop=True), reads=[QKT, KEYT], writes=[sb_])
                        c.op('dve', lambda e: e.tensor_copy(S2[:, g4 * 512:(g4 + 1) * 512], sb_[:]), reads=[sb_], writes=[S2])
                        yield
                    c.op('act', lambda e: e.copy(SM[:, 8:24], PB[3][:, 0:16]), reads=[PB[3]], writes=[SM])
                    rstd_from_ss(SM[:, 8:24], SM[:, 24:40], 128, SM)
                    yield
                    for hp2 in range(16):
                        sv = S2[:, hp2 * 128:(hp2 + 1) * 128]
                        c.op('dve', lambda e: e.max(out=TV[:, hp2 * 16:hp2 * 16 + 8], in_=sv), reads=[S2], writes=[TV])
                        c.op('dve', lambda e: e.max_index(out=TI[:, hp2 * 16:hp2 * 16 + 8], in_max=TV[:, hp2 * 16:hp2 * 16 + 8], in_values=sv), reads=[TV, S2], writes=[TI])
                        c.op('dve', lambda e: e.match_replace(out=SW[:], in_to_replace=TV[:, hp2 * 16:hp2 * 16 + 8], in_values=sv, imm_value=-1e30), reads=[TV, S2], writes=[SW])
                        yield
                        c.op('dve', lambda e: e.max(out=TV[:, hp2 * 16 + 8:hp2 * 16 + 16], in_=SW[:]), reads=[SW], writes=[TV])
                        c.op('dve', lambda e: e.max_index(out=TI[:, hp2 * 16 + 8:hp2 * 16 + 16], in_max=TV[:, hp2 * 16 + 8:hp2 * 16 + 16], in_values=SW[:]), reads=[TV, SW], writes=[TI])
                        yield
                    tv3 = TV[:].rearrange("p (a j) -> p a j", j=16)
                    c.op('dve', lambda e: e.tensor_tensor(out=tv3, in0=tv3, in1=bcl(SM[:, 24:40], 16), op=ALU.mult), reads=[TV, SM], writes=[TV])
                    c.op('dve', lambda e: e.tensor_copy(TIF[:], TI[:]), reads=[TI], writes=[TIF])
                    tv4 = TV[:].rearrange("p (h s j) -> p h s j", s=2, j=16)
                    v1 = tv4[:, :, 0, :]
                    v2 = tv4[:, :, 1, :]
                    CAND = S2
                    cand4 = CAND[:].rearrange("p (h a b) -> p h a b", a=16, b=16)
                    c.op('dve', lambda e: e.tensor_tensor(out=cand4, in0=v1[:, :, :, None].broadcast_to([128, 8, 16, 16]), in1=v2[:, :, None, :].broadcast_to([128, 8, 16, 16]), op=ALU.add), reads=[TV], writes=[S2])
                    yield
                    for h in range(8):
                        cv = CAND[:, h * 256:(h + 1) * 256]
                        c.op('dve', lambda e: e.max(out=SVT[:, h * 16:h * 16 + 8], in_=cv), reads=[S2], writes=[SVT])
                        c.op('dve', lambda e: e.max_index(out=CI[:, h * 16:h * 16 + 8], in_max=SVT[:, h * 16:h * 16 + 8], in_values=cv), reads=[SVT, S2], writes=[CI])
                        c.op('dve', lambda e: e.match_replace(out=CWK[:], in_to_replace=SVT[:, h * 16:h * 16 + 8], in_values=cv, imm_value=-1e30), reads=[SVT, S2], writes=[CWK])
                        yield
                        c.op('dve', lambda e: e.max(out=SVT[:, h * 16 + 8:h * 16 + 16], in_=CWK[:]), reads=[CWK], writes=[SVT])
                        c.op('dve', lambda e: e.max_index(out=CI[:, h * 16 + 8:h * 16 + 16], in_max=SVT[:, h * 16 + 8:h * 16 + 16], in_values=CWK[:]), reads=[SVT, CWK], writes=[CI])
                        yield
                    sv3 = SVT[:].rearrange("p (h k) -> p h k", k=16)
                    mx = SVT[:].rearrange("p (h k) -> p h k", k=16)[:, :, 0:1].broadcast_to([128, 8, 16])
                    g3 = GW[:].rearrange("p (h k) -> p h k", k=16)
                    c.op('dve', lambda e: e.tensor_tensor(out=g3, in0=sv3, in1=mx, op=ALU.subtract), reads=[SVT], writes=[GW])
                    c.op('act', lambda e: e.activation(out=GW[:], in_=GW[:], func=AF.Exp), reads=[GW], writes=[GW])
                    c.op('dve', lambda e: e.tensor_reduce(out=SM[:, 52:60], in_=g3, axis=AX.X, op=ALU.add), reads=[GW], writes=[SM])
                    c.op('dve', lambda e: e.reciprocal(out=SM[:, 52:60], in_=SM[:, 52:60]), reads=[SM], writes=[SM])
                    c.op('dve', lambda e: e.tensor_tensor(out=g3, in0=g3, in1=bcl(SM[:, 52:60], 16), op=ALU.mult), reads=[GW, SM], writes=[GW])
                    yield
                    c.op('dve', lambda e: e.tensor_single_scalar(out=CJ[:, 0:128], in_=CI[:], scalar=4, op=ALU.logical_shift_right), reads=[CI], writes=[CJ])
                    c.op('dve', lambda e: e.tensor_single_scalar(out=CJ[:, 128:256], in_=CI[:], scalar=15, op=ALU.bitwise_and), reads=[CI], writes=[CJ])
                    c.op('dve', lambda e: e.tensor_copy(CJF[:], CJ[:]), reads=[CJ], writes=[CJF])
                    tif4 = TIF[:].rearrange("p (h s j) -> p h s j", s=2, j=16)
                    EQ = TMP[0]
                    for s_ in range(2):
                        for hh4 in range(2):
                            hs = slice(hh4 * 4, hh4 * 4 + 4)
                            jf = CJF[:, s_ * 128:(s_ + 1) * 128].rearrange("p (h k) -> p h k", k=16)[:, hs, :]
                            eq4 = EQ[:].rearrange("p (h k j) -> p h k j", k=16, j=16)
                            io = IOTA16[:, 0:16]
                            iob = bass.AP(tensor=io.tensor, offset=io.offset, ap=[list(io.ap[0]), [0, 4], [0, 16], [1, 16]])
                            c.op('dve', lambda e: e.tensor_tensor(out=eq4, in0=jf[:, :, :, None].broadcast_to([128, 4, 16, 16]), in1=iob, op=ALU.is_equal), reads=[CJF, IOTA16], writes=[EQ])
                            c.op('dve', lambda e: e.tensor_tensor(out=eq4, in0=eq4, in1=tif4[:, hs, s_, :][:, :, None, :].broadcast_to([128, 4, 16, 16]), op=ALU.mult), reads=[EQ, TIF], writes=[EQ])
                            c.op('dve', lambda e: e.tensor_reduce(out=EF_[:, s_ * 128 + hh4 * 64:s_ * 128 + hh4 * 64 + 64].rearrange("p (h k) -> p h k", k=16), in_=eq4, axis=AX.X, op=ALU.add), reads=[EQ], writes=[EF_])
                            yield
                    c.op('dve', lambda e: e.scalar_tensor_tensor(out=EF_[:, 0:128], in0=EF_[:, 0:128], scalar=128.0, in1=EF_[:, 128:256], op0=ALU.mult, op1=ALU.add), reads=[EF_], writes=[EF_])
                    c.op('dve', lambda e: e.tensor_copy(EIDX[:], EF_[:, 0:128]), reads=[EF_], writes=[EIDX])
                    yield

                def back(t, p, fill):
                    r0 = t * 128
                    H2, H2d = H2P[p]
                    X1T, X1d = X1P[p]
                    EIDX, GW, HPRE = EIDXP[p], GWP[p], HPREP[p]
                    h2b = bass.AP(tensor=H2.tensor, offset=H2.offset, ap=[list(H2.ap[0]), [0, 4], [1, 1024]])
                    for g in range(32):
                        grp = GB[(g % 2) * 4:(g % 2) * 4 + 4]
                        for i in range(4):
                            sl = g * 4 + i
                            c.dma('pool', grp[i][:], peer_u[l], reads=[EIDX], writes=[grp[i]], indirect=bass.IndirectOffsetOnAxis(ap=EIDX[:, sl:sl + 1], axis=0))
                        gv = BIG[:, (g % 2) * 4096:(g % 2 + 1) * 4096].rearrange("p (s d) -> p s d", d=1024)
                        c.op('dve', lambda e: e.tensor_tensor(out=gv, in0=gv, in1=h2b, op=ALU.mult), reads=grp + H2d, writes=grp)
                        for i in range(4):
                            sl = g * 4 + i
                            c.op('act', lambda e: e.activation(out=grp[i][:], in_=grp[i][:], func=AF.Copy, accum_out=HPRE[:, sl:sl + 1]), reads=[grp[i]], writes=[grp[i], HPRE])
                        fill(1)

                    c.op('act', lambda e: e.activation(out=HPRE[:], in_=HPRE[:], func=AF.Gelu), reads=[HPRE], writes=[HPRE])
                    c.op('dve', lambda e: e.tensor_tensor(out=HPRE[:], in0=HPRE[:], in1=GW[:], op=ALU.mult), reads=[HPRE, GW], writes=[HPRE])
                    VR = [JUNKT, VR1]
                    for sl in range(128):
                        gb = GB[sl % 8]
                        vr = VR[sl % 2]
                        c.dma('pool', gb[:], peer_v[l], reads=[EIDX], writes=[gb], indirect=bass.IndirectOffsetOnAxis(ap=EIDX[:, sl:sl + 1], axis=0))
                        c.op('act', lambda e: e.activation(out=vr[:], in_=gb[:], func=AF.Copy, scale=HPRE[:, sl:sl + 1]), reads=[gb, HPRE], writes=[vr])
                        for half in range(2):
                            c.op('pe', lambda e: e.matmul(PB[5 + half][:], lhsT=IDB[:], rhs=vr[:, half * 512:(half + 1) * 512], start=(sl == 0), stop=(sl == 127)), reads=[IDB, vr], writes=[PB[5 + half]])
                        fill(1)
                    for half in range(2):
                        hs = slice(half * 512, (half + 1) * 512)
                        c.op('dve', lambda e: e.tensor_tensor(out=TMP[1][:, hs], in0=PB[5 + half][:], in1=MOD[:, 3 * D + half * 512:3 * D + (half + 1) * 512], op=ALU.mult), reads=[PB[5 + half], MOD], writes=[TMP[1]])
                    c.op('pool', lambda e: e.tensor_tensor(out=X1T, in0=X1T, in1=TMP[1][:], op=ALU.add), reads=X1d + [TMP[1]], writes=X1d)
                    c.dma('sp', xdst[r0:r0 + 128, :], X1T, reads=X1d, writes=[xdst_t])

                def advance(gen, n):
                    if gen is None:
                        return
                    for _ in range(n):
                        try:
                            next(gen)
                        except StopIteration:
                            return

                g0 = front(0, 0)
                advance(g0, 10 ** 6)
                for t in range(NT):
                    nxt = None
                    if t + 1 < NT - 1:
                        nxt = front(t + 1, (t + 1) % 2)
                    back(t, t % 2, (lambda n, _g=nxt: advance(_g, n)))
                    advance(nxt, 10 ** 6)
                    if t + 1 == NT - 1:
                        advance(front(t + 1, (t + 1) % 2), 10 ** 6)

        c.finish()
        print("ninst", c.ninst)
    return nc


_NC = None
import time as _time


def kernel(**inp):
    global _NC
    f = lambda a: np.ascontiguousarray(np.asarray(a, dtype=np.float32))
    x_prompt = f(inp["x_prompt"]); x_sample = f(inp["x_sample"])
    ohb = np.zeros((3, 32, 129), np.float32)
    for r, dil in enumerate(DILS):
        b = t5_bucket_np(np.arange(129) * dil)
        ohb[r, b, np.arange(129)] = 1.0
    shared = {}
    for k in ["rel_bias", "w_ada", "b_ada", "norm_mix", "norm_ffn", "w_in", "q_gain", "k_gain", "gmlp_norm", "w_s", "b_s",
              "conv_w", "conv_b", "w_a", "b_a", "w_x", "b_x", "lru_lambda", "out_gain", "w_out", "peer_wq"]:
        shared[k] = f(inp[k])
    shared["peer_keys"] = f(inp["peer_keys"]).reshape(DEPTH, 16, 128, 128)
    shared["ohb"] = ohb
    for i in range(DEPTH):
        shared[f"peer_u{i}"] = np.ascontiguousarray(f(inp["peer_u"])[i])
        shared[f"peer_v{i}"] = np.ascontiguousarray(f(inp["peer_v"])[i])
    shared["ohbr"] = np.ascontiguousarray(ohb[:, :, :0:-1])
    ck = f(inp["cache_k"]).reshape(DEPTH, 128, 2048, 512)
    cv = f(inp["cache_v"]).reshape(DEPTH, 128, 2048, 512)
    sc = f(inp["state_conv"]).reshape(DEPTH, 128, 768)
    sh = f(inp["state_h"])
    cp = f(inp["c_prompt"]); cs = f(inp["c_sample"])
    in_maps = []
    for c in range(8):
        b = c % 4
        xin = np.zeros((TOK, D), np.float32)
        xin[:SEQ] = x_prompt[b]
        xin[SEQ:SEQ + NS] = x_sample[c * NS:(c + 1) * NS, 0]
        cc = np.zeros((32, D), np.float32)
        cc[0] = cp[b]
        cc[1:1 + NS] = cs[c * NS:(c + 1) * NS]
        m = dict(shared)
        m.update(xin=xin, cc=cc,
                 cache_k=np.ascontiguousarray(ck[:, c * NS:(c + 1) * NS]), cache_v=np.ascontiguousarray(cv[:, c * NS:(c + 1) * NS]),
                 state_conv=np.ascontiguousarray(sc[:, c * NS:(c + 1) * NS]), state_h=np.ascontiguousarray(sh[:, c * NS:(c + 1) * NS]))
        in_maps.append(m)
    _t0 = _time.time()
    if _NC is None:
        _NC = build()
    print('[kernel] host prep+build done', round(_time.time() - _t0, 1), flush=True)
    res = run_bass_kernel_spmd(_NC, in_maps, core_ids=list(range(8)))
    R = res.results
    print('[kernel] launch done', round(_time.time() - _t0, 1), flush=True)
    y_prompt = np.stack([R[b]["y_out"][:SEQ] for b in range(4)])
    y_sample = np.concatenate([R[c]["y_out"][SEQ:SEQ + NS] for c in range(8)])[:, None, :]
    k_prompt = np.stack([R[b]["k_p"] for b in range(4)], axis=1).reshape(DEPTH, 4, 2048, 8, 64)
    v_prompt = np.stack([R[b]["v_p"] for b in range(4)], axis=1).reshape(DEPTH, 4, 2048, 8, 64)
    k_sample = np.concatenate([R[c]["k_s"] for c in range(8)], axis=1).reshape(DEPTH, 128, 1, 8, 64)
    v_sample = np.concatenate([R[c]["v_s"] for c in range(8)], axis=1).reshape(DEPTH, 128, 1, 8, 64)
    gv_prompt = np.stack([R[b]["gv_p"] for b in range(4)], axis=1)
    gv_sample = np.concatenate([R[c]["gv_s"] for c in range(8)], axis=1)[:, :, None, :]
    conv_prompt = np.stack([R[b]["conv_p"] for b in range(4)], axis=1)
    conv_sample = np.concatenate([R[c]["conv_s"] for c in range(8)], axis=1)
    h_prompt = np.stack([R[b]["h_p"] for b in range(4)], axis=1)
    h_sample = np.concatenate([R[c]["h_s"] for c in range(8)], axis=1)
    return tuple(np.ascontiguousarray(a, dtype=np.float32) for a in (
        y_prompt, y_sample, k_prompt, v_prompt, k_sample, v_sample, gv_prompt, gv_sample,
        conv_prompt, conv_sample, h_prompt, h_sample))
```
